# Optimizing a Trainium2 kernel written in Bass

```python
import math
import jax
import jax.numpy as jnp
from jax import lax
import numpy as np

D_MODEL = 1024
BATCH = 4
SEQ = 4096
DEPTH = 1
DEC_BATCH = 2
DEC_SEQ = 8192
PAST_LEN = 128

HEAD_DIM = 64
DA_HEADS = 8
DA_VDIM = 2 * HEAD_DIM
DIL_CONFIG = ((128, 1), (512, 4), (2048, 16))
N_DIL = 3
DIL_HEADS = 8
D_FF = 4 * D_MODEL
N_BUCKETS = 32
MAX_DISTANCE = 128
Q_BLOCK = 128
NEG_INF = -1e30
EPS = 1e-6

DA_QK = DA_HEADS * HEAD_DIM
DA_V = DA_HEADS * DA_VDIM
DIL_W = DIL_HEADS * HEAD_DIM
COL_SIZES = (DA_QK, DA_QK, DA_QK, DA_QK, DA_V) + (DIL_W,) * (3 * N_DIL) + (D_MODEL, D_MODEL)
IN_COLS = 4 * DA_QK + DA_V + 3 * N_DIL * DIL_W + 2 * D_MODEL
N_REL_HEADS = DA_HEADS + N_DIL * DIL_HEADS

kernel_name = "hybrid_diff_dilated_encoder"


def rms_norm(x, g):
    xf = x.astype(jnp.float32)
    y = xf * lax.rsqrt(jnp.mean(jnp.square(xf), axis=-1, keepdims=True) + EPS)
    return (y * g.astype(jnp.float32)).astype(x.dtype)


def rel_bucket(rel):
    nb = N_BUCKETS // 2
    max_exact = nb // 2
    ret = jnp.where(rel > 0, nb, 0)
    n = jnp.abs(rel)
    nf = jnp.maximum(n, 1).astype(jnp.float32)
    large = max_exact + (jnp.log(nf / max_exact) / math.log(MAX_DISTANCE / max_exact)
                         * (nb - max_exact)).astype(jnp.int32)
    large = jnp.minimum(large, nb - 1)
    return ret + jnp.where(n < max_exact, n, large)


def diff_attention(q1, q2, k1, k2, v, bias_tab, lam):
    B, H, S, Dh = q1.shape
    nblk = S // Q_BLOCK
    scale = Dh ** -0.5
    kpos = jnp.arange(S)

    def block(args):
        i, q1b, q2b = args
        qpos = i * Q_BLOCK + jnp.arange(Q_BLOCK)
        bias = jnp.transpose(bias_tab[rel_bucket(kpos[None, :] - qpos[:, None])], (2, 0, 1)).astype(jnp.float32)
        s1 = jnp.einsum('bhqd,bhkd->bhqk', q1b, k1).astype(jnp.float32) * scale + bias
        s2 = jnp.einsum('bhqd,bhkd->bhqk', q2b, k2).astype(jnp.float32) * scale + bias
        p = jax.nn.softmax(s1, axis=-1) - lam * jax.nn.softmax(s2, axis=-1)
        return jnp.einsum('bhqk,bhkv->bhqv', p.astype(v.dtype), v)

    qb1 = q1.reshape(B, H, nblk, Q_BLOCK, Dh).transpose(2, 0, 1, 3, 4)
    qb2 = q2.reshape(B, H, nblk, Q_BLOCK, Dh).transpose(2, 0, 1, 3, 4)
    out = lax.map(block, (jnp.arange(nblk), qb1, qb2))
    return out.transpose(1, 0, 3, 2, 4).reshape(B, S, H, v.shape[-1])


def dilated_window_attention(q, k, v, bias_tab, window, dilation):
    B, S, H, Dh = q.shape
    r = dilation
    half = window // (2 * r)
    L = S // r
    blk = half
    nb = -(-L // blk)
    Lp = nb * blk
    scale = Dh ** -0.5

    def to_sub(t):
        return t.reshape(B, L, r, H, Dh).transpose(0, 2, 1, 3, 4)

    qs = jnp.pad(to_sub(q), ((0, 0), (0, 0), (0, Lp - L), (0, 0), (0, 0))).reshape(B, r, nb, blk, H, Dh)

    def windows(t):
        t = jnp.pad(to_sub(t), ((0, 0), (0, 0), (blk, Lp - L + blk), (0, 0), (0, 0))).reshape(B, r, nb + 2, blk, H, Dh)
        return jnp.concatenate([t[:, :, 0:nb], t[:, :, 1:nb + 1], t[:, :, 2:nb + 2]], axis=3)

    kw, vw = windows(k), windows(v)
    qi = jnp.arange(blk)
    kj = jnp.arange(3 * blk) - blk
    rel = kj[None, :] - qi[:, None]
    kpos = jnp.arange(nb)[:, None] * blk + kj[None, :]
    valid = (jnp.abs(rel) <= half)[None] & ((kpos >= 0) & (kpos < L))[:, None, :]
    bias = jnp.transpose(bias_tab[rel_bucket(rel * r)], (2, 0, 1)).astype(jnp.float32)
    s = jnp.einsum('brnqhd,brnkhd->brnhqk', qs, kw).astype(jnp.float32) * scale + bias
    s = jnp.where(valid[:, None], s, NEG_INF)
    lse = jax.nn.logsumexp(s, axis=-1)
    p = jnp.exp(s - lse[..., None])
    out = jnp.einsum('brnhqk,brnkhd->brnqhd', p.astype(v.dtype), vw)
    out = out.reshape(B, r, Lp, H, Dh)[:, :, :L].transpose(0, 2, 1, 3, 4).reshape(B, S, H, Dh)
    lse = lse.transpose(0, 1, 2, 4, 3).reshape(B, r, Lp, H)[:, :, :L].transpose(0, 2, 1, 3).reshape(B, S, H)
    return out, lse


def encoder_layer(x, c, layer_idx, rel_bias, norm1_g, w_ada, b_ada, w_in, qn_a, kn_a,
                  lambda_q1, lambda_k1, lambda_q2, lambda_k2, subln_g, qn_b, kn_b,
                  w_br_a, w_br_b, w_o, norm2_g, w_up, w_down):
    B, S, _ = x.shape
    mod = jax.nn.silu(c) @ w_ada + b_ada
    shift1, scale1, gate1, shift2, scale2, gate2 = [m[:, None, :] for m in jnp.split(mod, 6, axis=-1)]

    h = rms_norm(x, norm1_g) * (1 + scale1) + shift1
    proj = h @ w_in
    splits = []
    acc = 0
    for size in COL_SIZES[:-1]:
        acc += size
        splits.append(acc)
    parts = jnp.split(proj, splits, axis=-1)
    q1, q2, k1, k2, va = parts[:5]
    dil_parts = parts[5:5 + 3 * N_DIL]
    gate_a_logit, gate_b_logit = parts[5 + 3 * N_DIL], parts[6 + 3 * N_DIL]

    def qk_heads(t, g):
        return rms_norm(t.reshape(B, S, -1, HEAD_DIM), g)

    lam_init = 0.8 - 0.6 * math.exp(-0.3 * layer_idx)
    f32 = jnp.float32
    lam = (jnp.exp(jnp.sum(lambda_q1.astype(f32) * lambda_k1.astype(f32)))
           - jnp.exp(jnp.sum(lambda_q2.astype(f32) * lambda_k2.astype(f32))) + lam_init)
    tr = lambda t: t.transpose(0, 2, 1, 3)
    ya = diff_attention(tr(qk_heads(q1, qn_a)), tr(qk_heads(q2, qn_a)),
                        tr(qk_heads(k1, kn_a)), tr(qk_heads(k2, kn_a)),
                        tr(va.reshape(B, S, DA_HEADS, DA_VDIM)), rel_bias[:, :DA_HEADS], lam)
    ya = rms_norm(ya, subln_g) * (1 - lam_init)
    ya = ya.reshape(B, S, DA_V) @ w_br_a

    outs, lses = [], []
    for g, (window, dilation) in enumerate(DIL_CONFIG):
        qg = qk_heads(dil_parts[3 * g], qn_b[g])
        kg = qk_heads(dil_parts[3 * g + 1], kn_b[g])
        vg = dil_parts[3 * g + 2].reshape(B, S, DIL_HEADS, HEAD_DIM)
        tab = rel_bias[:, DA_HEADS + g * DIL_HEADS: DA_HEADS + (g + 1) * DIL_HEADS]
        o, l = dilated_window_attention(qg, kg, vg, tab, window, dilation)
        outs.append(o)
        lses.append(l)
    wts = jax.nn.softmax(jnp.stack(lses, axis=0), axis=0)
    yb = jnp.sum(wts[..., None] * jnp.stack(outs, axis=0).astype(f32), axis=0).astype(x.dtype)
    yb = yb.reshape(B, S, DIL_W) @ w_br_b

    merged = jax.nn.sigmoid(gate_a_logit) * ya + jax.nn.sigmoid(gate_b_logit) * yb
    x = x + gate1 * (merged @ w_o)

    h2 = rms_norm(x, norm2_g) * (1 + scale2) + shift2
    x = x + gate2 * (jnp.square(jax.nn.relu(h2 @ w_up)) @ w_down)
    return x


def trunk(x, c, rel_bias, norm1_g, w_ada, b_ada, w_in, qn_a, kn_a, lambda_q1, lambda_k1,
          lambda_q2, lambda_k2, subln_g, qn_b, kn_b, w_br_a, w_br_b, w_o, norm2_g, w_up, w_down):
    for l in range(DEPTH):
        x = encoder_layer(x, c, l, rel_bias, norm1_g[l], w_ada[l], b_ada[l], w_in[l], qn_a[l], kn_a[l],
                          lambda_q1[l], lambda_k1[l], lambda_q2[l], lambda_k2[l], subln_g[l],
                          qn_b[l], kn_b[l], w_br_a[l], w_br_b[l], w_o[l], norm2_g[l], w_up[l], w_down[l])
    return x


def setup_inputs(seed: int = 0) -> dict:
    key = jax.random.key(seed)
    ks = jax.random.split(key, 24)
    nrm = lambda k, shape, s: jax.random.normal(k, shape, jnp.float32) * s
    D = D_MODEL
    return {
        "x_prompt": nrm(ks[0], (BATCH, SEQ, D), 1.0),
        "x_sample": nrm(ks[1], (DEC_BATCH, DEC_SEQ, D), 1.0),
        "c_prompt": nrm(ks[2], (BATCH, D), 1.0),
        "c_sample": nrm(ks[3], (DEC_BATCH, D), 1.0),
        "rel_bias": nrm(ks[4], (N_BUCKETS, N_REL_HEADS), 0.5),
        "norm1_g": 1.0 + nrm(ks[5], (DEPTH, D), 0.02),
        "w_ada": nrm(ks[6], (DEPTH, D, 6 * D), 0.2 * D ** -0.5),
        "b_ada": nrm(ks[7], (DEPTH, 6 * D), 0.02),
        "w_in": nrm(ks[8], (DEPTH, D, IN_COLS), D ** -0.5),
        "qn_a": 1.0 + nrm(ks[9], (DEPTH, HEAD_DIM), 0.02),
        "kn_a": 1.0 + nrm(ks[10], (DEPTH, HEAD_DIM), 0.02),
        "lambda_q1": nrm(ks[11], (DEPTH, HEAD_DIM), 0.1),
        "lambda_k1": nrm(ks[12], (DEPTH, HEAD_DIM), 0.1),
        "lambda_q2": nrm(ks[13], (DEPTH, HEAD_DIM), 0.1),
        "lambda_k2": nrm(ks[14], (DEPTH, HEAD_DIM), 0.1),
        "subln_g": 1.0 + nrm(ks[15], (DEPTH, DA_VDIM), 0.02),
        "qn_b": 1.0 + nrm(ks[16], (DEPTH, N_DIL, HEAD_DIM), 0.02),
        "kn_b": 1.0 + nrm(ks[17], (DEPTH, N_DIL, HEAD_DIM), 0.02),
        "w_br_a": nrm(ks[18], (DEPTH, DA_V, D), DA_V ** -0.5),
        "w_br_b": nrm(ks[19], (DEPTH, DIL_W, D), DIL_W ** -0.5),
        "w_o": nrm(ks[20], (DEPTH, D, D), D ** -0.5),
        "norm2_g": 1.0 + nrm(ks[21], (DEPTH, D), 0.02),
        "w_up": nrm(ks[22], (DEPTH, D, D_FF), D ** -0.5),
        "w_down": nrm(ks[23], (DEPTH, D_FF, D), D_FF ** -0.5),
    }


def reference(x_prompt, x_sample, c_prompt, c_sample, rel_bias, norm1_g, w_ada, b_ada, w_in,
              qn_a, kn_a, lambda_q1, lambda_k1, lambda_q2, lambda_k2, subln_g, qn_b, kn_b,
              w_br_a, w_br_b, w_o, norm2_g, w_up, w_down):
    y_prompt = trunk(x_prompt, c_prompt, rel_bias, norm1_g, w_ada, b_ada, w_in, qn_a, kn_a,
                     lambda_q1, lambda_k1, lambda_q2, lambda_k2, subln_g, qn_b, kn_b,
                     w_br_a, w_br_b, w_o, norm2_g, w_up, w_down)
    y_sample = trunk(x_sample, c_sample, rel_bias, norm1_g, w_ada, b_ada, w_in, qn_a, kn_a,
                     lambda_q1, lambda_k1, lambda_q2, lambda_k2, subln_g, qn_b, kn_b,
                     w_br_a, w_br_b, w_o, norm2_g, w_up, w_down)
    return (y_prompt, y_sample)
```

```python
import math
from contextlib import ExitStack
import numpy as np
import concourse.bass as bass
import concourse.mybir as mybir
from concourse.bass_utils import run_bass_kernel_spmd

F32 = mybir.dt.float32
BF16 = mybir.dt.bfloat16
AF = mybir.ActivationFunctionType
ALU = mybir.AluOpType
AX = mybir.AxisListType

D = 1024
SEQS = (4096, 8192)
SBASE = (0, 4096)
OWN = 2048
NEG = -30000.0
EPS = 1e-6
GW = 1280
DW = 384
OHW = 5 * GW + 3 * DW
DIL_R = (1, 4, 16)
SAME_ENGINE_SYNC = True
DBG = {}


class Buf:
    __slots__ = ("name", "w", "r", "ds", "multi", "wm", "pw")

    def __init__(self, name, multi=False):
        self.name = name
        self.w = None
        self.r = []
        self.ds = {}
        self.multi = multi
        self.wm = {}
        self.pw = []


class Prog:
    CE = ("pe", "act", "dve", "pool")
    ALLQ = ("pe", "act", "dve", "pool", "sp")
    ENGMAP = {"pe": "tensor", "act": "scalar", "dve": "vector", "pool": "gpsimd", "sp": "sync"}

    def __init__(self, nc):
        self.nc = nc
        self.q = {e: [] for e in self.ALLQ}
        self.sem = {e: nc.alloc_semaphore(name=f"pg_{e}") for e in self.CE}
        self.cnt = {e: 0 for e in self.CE}
        self.seen = {e: {} for e in self.ALLQ}
        self.dma_bufs = []
        self.pool = {"hw": [], "sw": []}
        self.final = {}
        self.nsem = 0
        self.ninst = 0

    def buf(self, name, multi=False):
        return Buf(name, multi)

    def _deps(self, eng, reads, writes, pwrites=()):
        deps = []
        for b in reads:
            if b.multi:
                deps.extend(b.wm.values())
            else:
                if b.w is not None:
                    deps.append(b.w)
                deps.extend(b.pw)
        for b in pwrites:
            if b.w is not None:
                deps.append(b.w)
            deps.extend(b.r)
        for b in writes:
            if b.multi:
                continue
            if b.w is not None:
                deps.append(b.w)
            deps.extend(b.pw)
            deps.extend(b.r)
        best = {}
        for (sem, val, te) in deps:
            if te == eng and not SAME_ENGINE_SYNC:
                continue
            if te == "pe" and eng == "pe":
                continue
            k = id(sem)
            if k not in best or best[k][1] < val:
                best[k] = (sem, val)
        waits = []
        seen = self.seen[eng]
        for k, (sem, val) in best.items():
            if seen.get(k, 0) >= val:
                continue
            seen[k] = val
            waits.append((sem, val))
        return waits

    def _commit(self, tok, reads, writes, pwrites=()):
        for b in reads:
            if not b.multi:
                b.r.append(tok)
        for b in pwrites:
            b.pw.append(tok)
            if len(b.pw) > 64:
                b.pw = b.pw[-64:]
            if len(b.r) > 64:
                b.r = b.r[-64:]
        for b in writes:
            if b.multi:
                b.wm[id(tok[0])] = tok
            else:
                b.w = tok
                b.r = []
                b.pw = []

    def op(self, eng, fn, reads=(), writes=(), pwrites=()):
        waits = self._deps(eng, reads, writes, pwrites)
        if self.cnt[eng] >= 30000:
            self.final[id(self.sem[eng])] = (self.sem[eng], self.cnt[eng])
            self.sem[eng] = self.nc.alloc_semaphore(name=f"pg_{eng}_{self.nsem}")
            self.nsem += 1
            self.cnt[eng] = 0
        self.cnt[eng] += 1
        tok = (self.sem[eng], self.cnt[eng], eng)
        self.q[eng].append((waits, fn, (self.sem[eng], 1)))
        self._commit(tok, reads, writes, pwrites)
        return tok

    def dma(self, eng, out, in_, reads=(), writes=(), owner=None, group=False, **kw):
        ob = owner if owner is not None else writes[0]
        kind = "sw" if eng == "pool" else "hw"
        saved = None
        if group and kind in ob.ds and len(writes) == 1 and writes[0].w is not None \
                and writes[0].w[0] is ob.ds[kind][0] and not writes[0].r:
            saved = writes[0].w
            writes[0].w = None
        waits = self._deps(eng, reads, writes)
        if saved is not None:
            writes[0].w = saved
        if kind not in ob.ds:
            if self.pool[kind]:
                ob.ds[kind] = list(self.pool[kind].pop())
            else:
                ob.ds[kind] = [self.nc.alloc_semaphore(name=f"d{self.nsem}"), 0]
                self.nsem += 1
            self.dma_bufs.append((ob, kind))
        ent = ob.ds[kind]
        ent[1] += 16
        tok = (ent[0], ent[1], None)
        self.final[id(ent[0])] = (ent[0], ent[1])

        def fn(e, out=out, in_=in_, kw=kw):
            return e.dma_start(out=out, in_=in_, **kw)

        self.q[eng].append((waits, fn, (ent[0], 16)))
        self._commit(tok, reads, writes)
        return tok

    def barrier(self, include_bg=False):
        bg_ids = set()
        if not include_bg:
            for (b, kind) in self.dma_bufs:
                if b.name.startswith("cv"):
                    bg_ids.add(id(b.ds[kind][0]))
        toks = [(self.sem[e], self.cnt[e]) for e in self.CE if self.cnt[e]]
        toks += [v for k_, v in self.final.items() if k_ not in bg_ids]
        for e in self.ALLQ:
            seen = self.seen[e]
            waits = []
            for (sem, val) in toks:
                if seen.get(id(sem), 0) >= val:
                    continue
                seen[id(sem)] = val
                waits.append((sem, val))
            if waits:
                self.q[e].append((waits, None, None))
        keep = []
        for (b, kind) in self.dma_bufs:
            if id(b.ds[kind][0]) in bg_ids:
                keep.append((b, kind))
            else:
                self.pool[kind].append(tuple(b.ds.pop(kind)))
        self.dma_bufs = keep

    def emit(self):
        nc = self.nc
        with nc.Block() as block:
            for e in self.ALLQ:
                items = self.q[e]
                if not items:
                    continue

                def body(engine, items=items):
                    for (waits, fn, inc) in items:
                        for (sem, val) in waits:
                            engine.wait_ge(sem, val)
                        if fn is not None:
                            fn(engine).then_inc(inc[0], inc[1])

                getattr(block, self.ENGMAP[e])(body)
                self.ninst += len(items)
        self.q = {e: [] for e in self.ALLQ}


def mm(P, out, pairs, reads, writes):
    def fn(e):
        n = len(pairs)
        ins = None
        for i, (l, r) in enumerate(pairs):
            ins = e.matmul(out, lhsT=l, rhs=r, start=(i == 0), stop=(i == n - 1))
        return ins
    P.op("pe", fn, reads, writes)


def mm_multi(P, groups, reads, writes):
    def fn(e):
        ins = None
        for (out, pairs) in groups:
            n = len(pairs)
            for i, (l, r) in enumerate(pairs):
                ins = e.matmul(out, lhsT=l, rhs=r, start=(i == 0), stop=(i == n - 1))
        return ins
    P.op("pe", fn, reads, writes)


def mm_raw(P, items, reads, writes, skip=False):
    def fn(e):
        ins = None
        for (o, l, r, st, sp) in items:
            if skip:
                ins = e.matmul(o, lhsT=l, rhs=r, start=st, stop=sp, skip_group_check=True)
            else:
                ins = e.matmul(o, lhsT=l, rhs=r, start=st, stop=sp)
        return ins
    P.op("pe", fn, reads, writes)


def transposes(P, items, ident, reads, writes):
    def fn(e):
        ins = None
        for (o, i) in items:
            ins = e.transpose(out=o, in_=i, identity=ident)
        return ins
    P.op("pe", fn, reads, writes)


def act(P, out, in_, func, reads, writes, pw=(), **kw):
    P.op("act", lambda e: e.activation(out=out, in_=in_, func=func, **kw), reads, writes, pw)


def ts(P, out, in0, s1, s2, op0, op1, reads, writes, eng="dve", pw=()):
    if s2 is None:
        P.op(eng, lambda e: e.tensor_scalar(out=out, in0=in0, scalar1=s1, scalar2=None, op0=op0), reads, writes, pw)
    else:
        P.op(eng, lambda e: e.tensor_scalar(out=out, in0=in0, scalar1=s1, scalar2=s2, op0=op0, op1=op1), reads, writes, pw)


def tt(P, out, in0, in1, op, reads, writes, eng="dve"):
    P.op(eng, lambda e: e.tensor_tensor(out=out, in0=in0, in1=in1, op=op), reads, writes)


def stt(P, out, in0, scalar, in1, op0, op1, reads, writes, eng="dve"):
    P.op(eng, lambda e: e.scalar_tensor_tensor(out=out, in0=in0, scalar=scalar, in1=in1, op0=op0, op1=op1), reads, writes)


def cp(P, out, in_, reads, writes, eng="dve", pw=()):
    P.op(eng, lambda e: e.tensor_copy(out=out, in_=in_), reads, writes, pw)


def mset(P, ap, val, writes, eng="dve"):
    P.op(eng, lambda e: e.memset(ap, val), (), writes)


class Ctx:
    pass


class _View:
    def __init__(self, ap):
        self.ap = ap

    def __getitem__(self, idx):
        return self.ap[idx]


class Slots:
    def __init__(self, K, es, name, shape, dt, n, psum=False):
        self.t = []
        self.b = []
        for i in range(n):
            if psum:
                nbytes = int(np.prod(shape[1:])) * (4 if dt == F32 else 2)
                assert nbytes <= 2048 or nbytes % 2048 == 0, (name, shape)
                if nbytes < 2048:
                    full = es.enter_context(K.nc.psum_tensor(f"{name}{i}", [128, 512 if dt == F32 else 1024], dt))
                    n_el = int(np.prod(shape[1:]))
                    t = full[:, 0:n_el]
                    if len(shape) == 3:
                        t = t.rearrange("p (a b) -> p a b", b=shape[2])
                    elif len(shape) == 4:
                        t = t.rearrange("p (a b c) -> p a b c", b=shape[2], c=shape[3])
                    t = _View(t)
                else:
                    t = es.enter_context(K.nc.psum_tensor(f"{name}{i}", shape, dt))
            else:
                t = es.enter_context(K.nc.sbuf_tensor(f"{name}{i}", shape, dt))
            self.t.append(t)
            self.b.append(K.P.buf(f"{name}{i}"))
        self.i = 0
        self.n = n

    def next(self):
        j = self.i % self.n
        self.i += 1
        return self.t[j], self.b[j]


def rstd_ops(K, ss, tmp, out, scale, reads, b_tmp, b_out):
    act(K.P, tmp, ss, AF.Ln, reads + [K.b_const], [b_tmp], scale=scale, bias=K.eps_c[:, 0:1])
    act(K.P, out, tmp, AF.Exp, [b_tmp], [b_out], scale=-0.5)


def phase0(K):
    nc, P = K.nc, K.P
    d = K.d
    bc = K.b_const
    with ExitStack() as es:
        sb = lambda name, shape, dt=F32: es.enter_context(nc.sbuf_tensor("s0_" + name, shape, dt))
        cv = [0]

        def conv(dst, src):
            cv[0] += 1
            P.dma("pool", dst, src, writes=[K.b_wscr], owner=P.buf(f"cv{cv[0]}"))
        w_in = d["w_in"]
        WB = d["WB"]
        for rb in range(4):
            rs = slice(rb * 256, (rb + 1) * 256)
            for (dst0, cA, cB) in ((0, 0, 512), (1024, 1024, 1536)):
                dv_ = WB[rs, dst0:dst0 + 1024].rearrange("p (h e) -> p h e", e=128)
                conv(dv_[:, :, 0:64], w_in[rs, cA:cA + 512].rearrange("p (h e) -> p h e", e=64))
                conv(dv_[:, :, 64:128], w_in[rs, cB:cB + 512].rearrange("p (h e) -> p h e", e=64))
            conv(WB[rs, 2048:9728], w_in[rs, 2048:9728])
        conv(d["WA"][:, :], d["w_br_a"][:, :])
        conv(d["WBB"][:, :], d["w_br_b"][:, :])
        conv(d["WO"][:, :], d["w_o"][:, :])
        for rb in range(4):
            conv(d["WUP"][rb * 256:(rb + 1) * 256, :], d["w_up"][rb * 256:(rb + 1) * 256, :])
            conv(d["WDN"][rb * 1024:(rb + 1) * 1024, :], d["w_down"][rb * 1024:(rb + 1) * 1024, :])
        b_idf, b_jf, b_blk = P.buf("idf"), P.buf("jf"), P.buf("blk")
        mset(P, K.ident_f[:], 0.0, [b_idf], eng="pool")
        P.op("pool", lambda e: e.affine_select(out=K.ident_f[:], in_=K.ident_f[:], pattern=[[-1, 128]], compare_op=ALU.not_equal,
                                               fill=1.0, base=0, channel_multiplier=1), [b_idf], [b_idf])
        mset(P, K.J_f[:], 0.0, [b_jf], eng="pool")
        P.op("pool", lambda e: e.affine_select(out=K.J_f[:], in_=K.J_f[:], pattern=[[1, 128]], compare_op=ALU.not_equal,
                                               fill=1.0, base=-127, channel_multiplier=1), [b_jf], [b_jf])
        mset(P, K.ones_f[:], 1.0, [bc])
        mset(P, K.eps_c[:], EPS, [bc])
        mset(P, K.blk_f[:], 0.0, [b_blk])
        mset(P, K.blk_f[0:64, 0:64], 1.0, [b_blk])
        mset(P, K.blk_f[64:128, 64:128], 1.0, [b_blk])
        relb = sb("relb", [33, 32])
        oh = sb("oh", [33, OHW])
        ohfar = sb("ohfar", [32, 384])
        g1 = sb("g1", [128, 8])
        g2 = sb("g2", [128, 8])
        bada_pp = sb("bada_pp", [128, 48])
        bada_row = sb("bada_row", [1, 6 * D])
        cpp = sb("cpp", [128, 2, 8])
        lam4 = sb("lam4", [128, 4, 64])
        sgt = sb("sgt", [128, 128])
        ld = lambda out, in_, **kw: P.dma("sp", out, in_, writes=[bc], **kw)
        mset(P, relb[32:33, :], NEG, [bc])
        ld(relb[0:32, :], d["rel_bias"][:, :])
        ld(oh[:], d["oh"][:, :])
        ld(ohfar[:], d["ohfar"][:, :])
        ld(g1[:], d["norm1_g"].rearrange("o (c p) -> p (o c)", p=128), allow_slow_non_contiguous=True)
        ld(g2[:], d["norm2_g"].rearrange("o (c p) -> p (o c)", p=128), allow_slow_non_contiguous=True)
        ld(bada_pp[:], d["b_ada"].rearrange("o (j p) -> p (o j)", p=128), allow_slow_non_contiguous=True)
        ld(bada_row[:], d["b_ada"][:, :])
        ld(cpp[:], d["c2"].rearrange("s (c p) -> p s c", p=128), allow_slow_non_contiguous=True)
        for i, nm in enumerate(("lambda_q1", "lambda_k1", "lambda_q2", "lambda_k2")):
            ld(lam4[:, i, :], bass.AP(d[nm].tensor, 0, [[0, 128], [1, 64]]))
        ld(sgt[:], bass.AP(d["subln_g"].tensor, 0, [[0, 128], [1, 128]]))
        gsrc = [("qn_a", 0), ("kn_a", 0), ("qn_b", 0), ("kn_b", 0), ("qn_b", 1), ("kn_b", 1), ("qn_b", 2), ("kn_b", 2)]
        for i, (nm, row) in enumerate(gsrc):
            src = bass.AP(d[nm].tensor, row * 64, [[1, 64], [1, 1]])
            ld(K.gains[0:64, i:i + 1], src)
            ld(K.gains[64:128, i:i + 1], src)
        ld(K.valid_c[:], d["valid"].rearrange("s (n p) -> p s n", p=128), allow_slow_non_contiguous=True)
        P.barrier()
        cp(P, K.ident_b[:], K.ident_f[:], [b_idf], [bc])
        cp(P, K.blk_b[:], K.blk_f[:], [b_blk], [bc])
        ts(P, K.sg_bc[:], sgt[:], 0.8, None, ALU.mult, None, [bc], [bc])
        lp = sb("lp", [128, 2, 64])
        ls = sb("ls", [128, 2])
        le = sb("le", [128, 2])
        tt(P, lp[:, 0, :], lam4[:, 0, :], lam4[:, 1, :], ALU.mult, [bc], [bc])
        tt(P, lp[:, 1, :], lam4[:, 2, :], lam4[:, 3, :], ALU.mult, [bc], [bc])
        P.op("dve", lambda e: e.reduce_sum(out=ls[:, 0:1], in_=lp[:, 0, :], axis=AX.X), [bc], [bc])
        P.op("dve", lambda e: e.reduce_sum(out=ls[:, 1:2], in_=lp[:, 1, :], axis=AX.X), [bc], [bc])
        act(P, le[:], ls[:], AF.Exp, [bc], [bc])
        tt(P, K.neglam[:], le[:, 1:2], le[:, 0:1], ALU.subtract, [bc], [bc])
        ts(P, K.neglam[:], K.neglam[:], -0.2, None, ALU.add, None, [bc], [bc])
        with ExitStack() as es2:
            gsb = es2.enter_context(nc.sbuf_tensor("gsb", [32, OHW], F32))
            pg = Slots(K, es2, "pg", [128, 512], F32, 2, psum=True)
            c0 = 0
            while c0 < OHW:
                w = min(512, OHW - c0)
                pt, pb = pg.next()
                mm(P, pt[0:32, 0:w], [(relb[0:33, :], oh[0:33, c0:c0 + w])], [bc], [pb])
                cp(P, gsb[:, c0:c0 + w], pt[0:32, 0:w], [pb], [bc])
                c0 += w
            P.dma("sp", d["G"][:, :], gsb[:], reads=[bc], writes=[K.b_scr], owner=bc)
            rbs = Slots(K, es2, "rbh", [32, 128], F32, 2)
            for h in range(8):
                rb, brb = rbs.next()
                ts(P, rb[:], K.ones_f[0:32, :], relb[0:32, h:h + 1], None, ALU.mult, None, [bc], [brb])
                pt, pb = pg.next()
                mm(P, pt[:, 0:384], [(rb[:], ohfar[:])], [bc, brb], [pb])
                cp(P, K.farB[:, h, :], pt[:, 0:384], [pb], [bc])
            P.barrier()
            P.emit()
        with ExitStack() as es2:
            sc = es2.enter_context(nc.sbuf_tensor("sc", [128, 2, 8], F32))
            scb = es2.enter_context(nc.sbuf_tensor("scb", [128, 2, 8, 128], F32))
            modpp = es2.enter_context(nc.sbuf_tensor("modpp", [128, 4, 8, 2], F32))
            wsl = Slots(K, es2, "wada", [128, 8, D], F32, 2)
            pm_s = Slots(K, es2, "pm", [128, 4, 8, 2], F32, 1, psum=True)
            pm, b_pm = pm_s.next()
            pgt = Slots(K, es2, "pgt", [128, 512], F32, 2, psum=True)
            act(P, sc[:], cpp[:], AF.Silu, [bc], [bc])
            for s in range(2):
                for kc in range(8):
                    ts(P, scb[:, s, kc, :], K.ones_f[:], sc[:, s, kc:kc + 1], None, ALU.mult, None, [bc], [bc])
            vmap = {0: 0, 1: 1, 3: 2, 4: 3}
            for v in range(6):
                wt, wb = wsl.next()
                for hh in range(2):
                    P.dma("sp", wt[:, hh * 4:(hh + 1) * 4, :],
                          d["w_ada"][hh * 512:(hh + 1) * 512, v * D:(v + 1) * D].rearrange("(c p) n -> p c n", p=128),
                          writes=[wb])
                if v in vmap:
                    vi = vmap[v]
                    groups = []
                    for fc in range(8):
                        groups.append((pm[:, vi, fc, :],
                                       [(wt[:, kc, fc * 128:(fc + 1) * 128], sc[:, :, kc]) for kc in range(8)]))
                    mm_multi(P, groups, [wb, bc], [b_pm])
                else:
                    gi = 0 if v == 2 else 1
                    for s in range(2):
                        for half in range(2):
                            pt, pb = pgt.next()
                            pairs = [(scb[:, s, kc, :], wt[:, kc, half * 512:(half + 1) * 512]) for kc in range(8)]
                            pairs.append((K.ones_f[0:1, :], bada_row[0:1, v * D + half * 512: v * D + (half + 1) * 512]))
                            mm(P, pt[:], pairs, [wb, bc], [pb])
                            cp(P, K.gate_bc[:, s, gi, half * 512:(half + 1) * 512], pt[:], [pb], [bc])
            cp(P, modpp[:], pm[:], [b_pm], [bc])
            for s in range(2):
                for (vi, v, dst, g) in ((0, 0, K.B1, None), (1, 1, K.A1, g1), (2, 3, K.B2, None), (3, 4, K.A2, g2)):
                    tt(P, dst[:, s, :], modpp[:, vi, :, s], bada_pp[:, v * 8:(v + 1) * 8], ALU.add, [bc], [bc])
                    if g is not None:
                        stt(P, dst[:, s, :], dst[:, s, :], 1.0, g[:], ALU.add, ALU.mult, [bc], [bc])
            P.barrier()
            P.emit()


def norm_to_hT(K, xt, bx, work, s, A, B, hT_dst, b_hT, col0):
    P = K.P
    bc = K.b_const
    sq, bsq = work["sq"].next()
    st, bst = work["st"].next()
    xn, bxn = work["xn"].next()
    pT, bpT = work["pT"].next()
    mset(P, st[:, 0:1], 0.0, [bst])
    act(P, sq[:], xt, AF.Square, [bx], [bsq, bst], accum_out=st[:, 0:1])
    act(P, st[:, 1:2], st[:, 0:1], AF.Ln, [bst, bc], [bst], scale=1.0 / D, bias=K.eps_c[:, 0:1])
    act(P, st[:, 2:3], st[:, 1:2], AF.Exp, [bst], [bst], scale=-0.5)
    ts(P, xn[:], xt, st[:, 2:3], None, ALU.mult, None, [bx, bst], [bxn])
    transposes(P, [(pT[:, kc, :], xn[:, kc * 128:(kc + 1) * 128]) for kc in range(8)], K.ident_b[:], [bxn, bc], [bpT])
    for kc in range(8):
        o = hT_dst[:, kc, col0:col0 + 128]
        if kc % 2 == 0:
            act(P, o, pT[:, kc, :], AF.Identity, [bpT, bc], [b_hT], scale=A[:, s, kc:kc + 1], bias=B[:, s, kc:kc + 1])
        else:
            ts(P, o, pT[:, kc, :], A[:, s, kc:kc + 1], B[:, s, kc:kc + 1], ALU.mult, ALU.add, [bpT, bc], [b_hT])


def norm_work(K, es, pfx, depth=2):
    return {
        "sq": Slots(K, es, pfx + "sq", [128, D], BF16, 2),
        "st": Slots(K, es, pfx + "st", [128, 4], F32, depth + 2),
        "xn": Slots(K, es, pfx + "xn", [128, D], BF16, depth),
        "pT": Slots(K, es, pfx + "pT", [128, 8, 128], BF16, depth, psum=True),
    }


def bias_tile_jobs(K, es2):
    nc, P, d = K.nc, K.P, K.d
    bc = K.b_const
    G = d["G"]
    hs = Slots(K, es2, "hk", [128, 8, 512], F32, 2)
    bo = Slots(K, es2, "bo", [128, 8, 512], F32, 2)
    pf = Slots(K, es2, "pf", [128, 512], F32, 4, psum=True)
    hkd = Slots(K, es2, "hkd", [128, 8, 2, 128], F32, 2)
    bod = Slots(K, es2, "bod", [128, 8, 2, 128], F32, 2)
    jobs = []

    def head_job(s, h):
        ht, hb = hs.next()
        P.dma("sp", ht[:, 0:6, :], bass.AP(G.tensor, h * OHW, [[1, 128], [128, 6], [1, 512]]), writes=[hb])
        P.dma("sp", ht[:, 6, :], bass.AP(G.tensor, h * OHW + (1 + 2 * s) * GW + 640, [[1, 128], [1, 512]]), writes=[hb], group=True)
        P.dma("sp", ht[:, 7, :], bass.AP(G.tensor, h * OHW + (2 + 2 * s) * GW + 0, [[1, 128], [1, 512]]), writes=[hb], group=True)
        ot, ob = bo.next()
        for i in range(8):
            pt, pb = pf.next()
            mm(P, pt[:], [(K.J_f[:], ht[:, i, :])], [hb, bc], [pb])
            cp(P, ot[:, i, :], pt[:], [pb], [ob])
        P.dma("pool", d["BM"][s][h], ot[:], reads=[ob], writes=[K.b_scr], owner=ob)

    def dil_job(g):
        ht, hb = hkd.next()
        for hh in range(8):
            head = 8 + 8 * g + hh
            P.dma("sp", ht[:, hh, :, :], bass.AP(G.tensor, head * OHW + 5 * GW + g * DW, [[1, 128], [128, 2], [1, 128]]), writes=[hb],
                  group=True)
        ot, ob = bod.next()
        for i in range(4):
            pt, pb = pf.next()
            mm(P, pt[:], [(K.J_f[:], ht[:, 2 * i:2 * i + 2, :, :].rearrange("p a b c -> p (a b c)"))], [hb, bc], [pb])
            cp(P, ot[:, 2 * i:2 * i + 2, :, :].rearrange("p a b c -> p (a b c)"), pt[:], [pb], [ob])
        P.dma("pool", d["TB"][g], ot[:], reads=[ob], writes=[K.b_scr], owner=ob)

    for s in range(2):
        for h in range(8):
            jobs.append(lambda s=s, h=h: head_job(s, h))
    for g in range(3):
        jobs.append(lambda g=g: dil_job(g))
    return jobs


def phaseA(K):
    nc, P, d = K.nc, K.P, K.d
    bc = K.b_const
    with ExitStack() as es:
        xs = Slots(K, es, "xa", [128, D], F32, 4)
        hs = Slots(K, es, "hTa", [128, 8, 512], BF16, 3)
        work = norm_work(K, es, "a", depth=3)
        bjobs = bias_tile_jobs(K, es)
        jobs = []
        for s in range(2):
            for c in range(SEQS[s] // 512):
                for j in range(4):
                    jobs.append({"s": s, "c": c, "j": j})
        cur = {}

        def st0(i):
            jb = jobs[i]
            s, c, j = jb["s"], jb["c"], jb["j"]
            if j == 0:
                cur[(s, c)] = hs.next()
            if j == 2 and bjobs:
                bjobs.pop(0)()
            xt, bx = xs.next()
            r0 = SBASE[s] + c * 512 + j * 128
            P.dma("sp", xt[:], d["xseq"][r0:r0 + 128, :], writes=[bx])
            sq, bsq = work["sq"].next()
            st, bst = work["st"].next()
            xn, bxn = work["xn"].next()
            mset(P, st[:, 0:1], 0.0, [bst])
            act(P, sq[:], xt[:], AF.Square, [bx], [bsq, bst], accum_out=st[:, 0:1])
            act(P, st[:, 1:2], st[:, 0:1], AF.Ln, [bst, bc], [bst], scale=1.0 / D, bias=K.eps_c[:, 0:1])
            act(P, st[:, 2:3], st[:, 1:2], AF.Exp, [bst], [bst], scale=-0.5)
            ts(P, xn[:], xt[:], st[:, 2:3], None, ALU.mult, None, [bx, bst], [bxn])
            jb["xn"] = (xn, bxn)

        def st1(i):
            jb = jobs[i]
            xn, bxn = jb["xn"]
            pT, bpT = work["pT"].next()
            transposes(P, [(pT[:, kc, :], xn[:, kc * 128:(kc + 1) * 128]) for kc in range(8)], K.ident_b[:], [bxn, bc], [bpT])
            jb["pT"] = (pT, bpT)

        def st2(i):
            jb = jobs[i]
            s, c, j = jb["s"], jb["c"], jb["j"]
            pT, bpT = jb["pT"]
            ht, hb = cur[(s, c)]
            for kc in range(8):
                o = ht[:, kc, j * 128:(j + 1) * 128]
                if j % 2 == 0:
                    act(P, o, pT[:, kc, :], AF.Identity, [bpT, bc], [], pw=[hb], scale=K.A1[:, s, kc:kc + 1], bias=K.B1[:, s, kc:kc + 1])
                else:
                    ts(P, o, pT[:, kc, :], K.A1[:, s, kc:kc + 1], K.B1[:, s, kc:kc + 1], ALU.mult, ALU.add, [bpT, bc], [], pw=[hb])
            if j == 3:
                P.dma("pool", d["hT"][s][:, :, c * 512:(c + 1) * 512], ht[:], reads=[hb], writes=[K.b_scr], owner=hb)

        run_pipeline(len(jobs), [st0, None, st1, None, st2])
        while bjobs:
            bjobs.pop(0)()
        P.barrier(include_bg=True)
        P.emit()


def phaseB(K):
    nc, P, d = K.nc, K.P, K.d
    bc = K.b_const
    w_in = d["w_in"]
    with ExitStack() as es:
        wsl = Slots(K, es, "wb", [128, 8, D], BF16, 2)
        hsl = Slots(K, es, "hTb", [128, 8, 512], BF16, 3)
        pp = Slots(K, es, "ppb", [128, 512], F32, 3, psum=True)
        pss = Slots(K, es, "pss", [128, 512], F32, 2, psum=True)
        sqs = Slots(K, es, "sqb", [128, 512], BF16, 2)
        lns = Slots(K, es, "lnb", [128, 512], F32, 2)
        ost = Slots(K, es, "ostb", [128, 512], BF16, 6)
        vas = Slots(K, es, "vas", [128, 8, 130], BF16, 4)
        dvs = Slots(K, es, "dvs", [128, 8, 66], BF16, 4)
        gst = Slots(K, es, "gst", [128, 512], F32, 5)
        for t, b in zip(vas.t, vas.b):
            mset(P, t[:, :, 128:130], 1.0, [b])

        WB = d["WB"]

        def load_w(wt, wb, col0, n, dst0=0):
            for hh in range(2):
                P.dma("sp", wt[:, hh * 4:(hh + 1) * 4, dst0:dst0 + n],
                      WB[hh * 512:(hh + 1) * 512, col0:col0 + n].rearrange("(c p) n -> p c n", p=128), reads=[K.b_wscr], writes=[wb], group=True)

        def load_w_pairs(wt, wb, colA, colB):
            load_w(wt, wb, 0 if colA == 0 else 1024, 1024)

        def load_h(s, c):
            ht, hb = hsl.next()
            P.dma("sp", ht[:], d["hT"][s][:, :, c * 512:(c + 1) * 512], writes=[hb])
            return ht, hb

        jobs = []
        slabs = []

        def add_slab(loader):
            slabs.append(loader)
            return len(slabs) - 1

        NCH = [SEQS[s] // 512 for s in range(2)]
        WCH = [[(c % NCH[s]) for c in range(-2, 6)] for s in range(2)]
        sl = add_slab(lambda wt, wb: load_w_pairs(wt, wb, 1024, 1536))
        for s in range(2):
            for c in range(NCH[s]):
                for h in range(8):
                    jobs.append({"k": "fm", "sl": sl, "s": s, "c": c, "tcol": h * 128, "gain": 1,
                                 "dst": d["KT"][s][h, :, c * 512:(c + 1) * 512]})
        sl = add_slab(lambda wt, wb: load_w(wt, wb, 2048, 1024))
        for s in range(2):
            for c in range(NCH[s]):
                for j in range(4):
                    for half in range(2):
                        jobs.append({"k": "va", "sl": sl, "s": s, "c": c, "j": j, "half": half})
        sl = add_slab(lambda wt, wb: load_w_pairs(wt, wb, 0, 512))
        for s in range(2):
            for c in range(4):
                for h in range(8):
                    jobs.append({"k": "fm", "sl": sl, "s": s, "c": c, "tcol": h * 128, "gain": 0,
                                 "dst": d["QT"][s][h, :, c * 512:(c + 1) * 512]})
        for g in range(3):
            base = 3072 + 1536 * g
            sl = add_slab(lambda wt, wb, base=base: load_w(wt, wb, base, 1024))
            for s in range(2):
                for wi, c in enumerate(WCH[s]):
                    for hp in range(4):
                        jobs.append({"k": "fm", "sl": sl, "s": s, "c": c, "tcol": 512 + hp * 128, "gain": 3 + 2 * g,
                                     "dst": d["DK"][s][g, hp, :, wi * 512:(wi + 1) * 512]})
                        if 2 <= wi < 6:
                            jobs.append({"k": "fm", "sl": sl, "s": s, "c": c, "tcol": hp * 128, "gain": 2 + 2 * g,
                                         "dst": d["DQ"][s][g, hp, :, (wi - 2) * 512:(wi - 1) * 512]})
            sl = add_slab(lambda wt, wb, base=base: load_w(wt, wb, base + 1024, 512))
            for s in range(2):
                for wi, c in enumerate(WCH[s]):
                    for j in range(4):
                        jobs.append({"k": "dv", "sl": sl, "s": s, "c": c, "j": j, "wi": wi, "g": g})
        for gi in range(2):
            sl = add_slab(lambda wt, wb, gi=gi: load_w(wt, wb, 7680 + gi * 1024, 1024))
            for s in range(2):
                for c in range(4):
                    for j in range(4):
                        for half in range(2):
                            jobs.append({"k": "gt", "sl": sl, "s": s, "c": c, "j": j, "half": half, "gi": gi})

        slab_t = {}
        chunk_t = {}
        state = {"va": None}

        def get_slab(sl):
            if sl not in slab_t:
                wt, wb = wsl.next()
                slabs[sl](wt, wb)
                slab_t[sl] = (wt, wb)
            return slab_t[sl]

        ckeys = []
        for jb in jobs:
            key = (jb["sl"], jb["s"], jb["c"])
            if not ckeys or ckeys[-1] != key:
                ckeys.append(key)
            jb["ck"] = len(ckeys) - 1

        def get_chunk(ck):
            for k_ in (ck, ck + 1):
                if k_ < len(ckeys) and k_ not in chunk_t:
                    chunk_t[k_] = load_h(ckeys[k_][1], ckeys[k_][2])
            chunk_t.pop(ck - 1, None)
            return chunk_t[ck]

        def st0(i):
            jb = jobs[i]
            wt, wb = get_slab(jb["sl"])
            if jb["sl"] + 1 < len(slabs) and (i == 0 or jobs[i - 1]["sl"] != jb["sl"]):
                pass
            ht, hb = get_chunk(jb["ck"])
            pt, pb = pp.next()
            k = jb["k"]
            if k == "fm":
                tcol = jb["tcol"]
                mm(P, pt[:], [(wt[:, kc, tcol:tcol + 128], ht[:, kc, :]) for kc in range(8)], [wb, hb], [pb])
            elif k == "dv":
                j = jb["j"]
                mm(P, pt[:], [(ht[:, kc, j * 128:(j + 1) * 128], wt[:, kc, 0:512]) for kc in range(8)], [wb, hb], [pb])
            else:
                j, half = jb["j"], jb["half"]
                mm(P, pt[:], [(ht[:, kc, j * 128:(j + 1) * 128], wt[:, kc, half * 512:(half + 1) * 512]) for kc in range(8)],
                   [wb, hb], [pb])
            jb["pp"] = (pt, pb)
            if i + 1 < len(jobs) and jb["sl"] + 1 < len(slabs) and (i == 0 or jobs[i - 1]["sl"] != jb["sl"]):
                get_slab(jb["sl"] + 1)

        def st1(i):
            jb = jobs[i]
            pt, pb = jb["pp"]
            k = jb["k"]
            s = jb["s"]
            if k == "fm":
                sq, bsq = sqs.next()
                act(P, sq[:], pt[:], AF.Square, [pb], [bsq])
                ps_, bps = pss.next()
                mm(P, ps_[:], [(K.blk_b[:], sq[:])], [bsq, bc], [bps])
                jb["ps"] = (ps_, bps)
            elif k == "va":
                j, half, c = jb["j"], jb["half"], jb["c"]
                if half == 0:
                    state["va"] = vas.next()
                va, bva = state["va"]
                src = pt[:].rearrange("p (h e) -> p h e", e=128)
                if half == 0:
                    act(P, va[:, 0:4, 0:128], src, AF.Copy, [pb], [], pw=[bva])
                else:
                    cp(P, va[:, 4:8, 0:128], src, [pb], [], pw=[bva])
                    r0 = c * 512 + j * 128
                    P.dma("pool", d["VA"][s][r0:r0 + 128, :], va[:].rearrange("p h e -> p (h e)"), reads=[bva], writes=[K.b_scr], owner=bva)
            elif k == "dv":
                j, wi, g = jb["j"], jb["wi"], jb["g"]
                dv, bdv = dvs.next()
                vcol = K.valid_c[:, s, wi * 4 + j: wi * 4 + j + 1]
                ts(P, dv[:, :, 0:64], pt[:].rearrange("p (h e) -> p h e", e=64), vcol, None, ALU.mult, None, [pb, bc], [bdv])
                ts(P, dv[:, :, 64:66], K.ones_f[:, 0:16].rearrange("p (h e) -> p h e", e=2), vcol, None, ALU.mult, None, [bc], [bdv],
                   eng="dve")
                r0 = wi * 512 + j * 128
                P.dma("pool", d["DV"][s][g, r0:r0 + 128, :], dv[:].rearrange("p h e -> p (h e)"), reads=[bdv], writes=[K.b_scr], owner=bdv)
            else:
                j, half, c, gi = jb["j"], jb["half"], jb["c"], jb["gi"]
                go, bgo = gst.next()
                act(P, go[:], pt[:], AF.Sigmoid, [pb], [bgo])
                r0 = c * 512 + j * 128
                cc = gi * 1024 + half * 512
                P.dma("pool", d["GT"][s][r0:r0 + 128, cc:cc + 512], go[:], reads=[bgo], writes=[K.b_scr], owner=bgo)

        def st2(i):
            jb = jobs[i]
            if jb["k"] != "fm":
                return
            pt, pb = jb["pp"]
            ps_, bps = jb["ps"]
            ln, bln = lns.next()
            act(P, ln[:], ps_[:], AF.Ln, [bps, bc], [bln], scale=1.0 / 64, bias=K.eps_c[:, 0:1])
            act(P, ln[:], ln[:], AF.Exp, [bln], [bln], scale=-0.5)
            o, bo = ost.next()
            g = jb["gain"]
            stt(P, o[:], pt[:], K.gains[:, g:g + 1], ln[:], ALU.mult, ALU.mult, [pb, bln, bc], [bo])
            P.dma("pool", jb["dst"], o[:], reads=[bo], writes=[K.b_scr], owner=bo)

        run_pipeline(len(jobs), [st0, st1, st2])
        P.barrier()
        P.emit()


def run_pipeline(njobs, stages, deferred=None):
    ns = len(stages)
    for tick in range(njobs + ns - 1 + 24):
        for st in range(ns):
            j = tick - st
            if 0 <= j < njobs and stages[st] is not None:
                stages[st](j)
        if deferred is not None:
            for fn in deferred.pop(tick, []):
                fn()
    assert not deferred, deferred.keys()


def phaseC(K):
    nc, P, d = K.nc, K.P, K.d
    bc = K.b_const
    G = d["G"]
    with ExitStack() as es:
        kts = Slots(K, es, "ktc", [128, 8192], BF16, 2)
        qts = Slots(K, es, "qtc", [128, 2, OWN], BF16, 2)
        for t_, b_ in zip(qts.t, qts.b):
            mset(P, t_[64:128, 0, :], 0.0, [b_], eng="pool")
            mset(P, t_[0:64, 1, :], 0.0, [b_], eng="pool")
        vts = Slots(K, es, "vac", [128, 64, 130], BF16, 2)
        bms = Slots(K, es, "bmx", [128, 8, 512], F32, 2)
        ps12 = Slots(K, es, "ps12", [128, 1024], F32, 2, psum=True)
        pacc_s = Slots(K, es, "pacc", [128, 3, 130], F32, 3, psum=True)
        pacc = pacc_s.t
        b_acc = P.buf("pacc")
        ptr = Slots(K, es, "ptrc", [128, 128], BF16, 1, psum=True)
        p12s = Slots(K, es, "p12s", [128, 1024], BF16, 5)
        tm12 = Slots(K, es, "tm12", [128, 1024], F32, 3)
        accs = Slots(K, es, "accs", [128, 9, 130], F32, 2)
        fin = Slots(K, es, "finc", [128, 20], F32, 8)
        ofs = Slots(K, es, "ofs", [128, 128], F32, 4)
        of2 = Slots(K, es, "of2", [128, 128], F32, 2)
        rds = Slots(K, es, "rdc", [128, 8], F32, 2)
        osq = Slots(K, es, "osq", [128, 128], BF16, 2)
        onb = Slots(K, es, "onb", [128, 128], BF16, 8)

        yst = Slots(K, es, "ystc", [128, 512], BF16, 2)
        jobs = []
        for s in range(2):
            n = SEQS[s] // 128
            for h in range(8):
                if DBG.get("c_heads") is not None and s * 8 + h >= DBG["c_heads"]:
                    continue
                for t in range(4):
                    far_ = list(range(6, n))
                    hf = len(far_) // 2
                    order = far_[:hf] + list(range(6)) + far_[hf:]
                    assert sorted(order) == list(range(n))
                    for pos, jp in enumerate(order):
                        jobs.append({"s": s, "h": h, "t": t, "jp": jp, "n": n, "kb": (4 * t - 1 + jp) % n,
                                     "first": pos == 0, "last": pos == n - 1})
        head = {}
        deferred = {}

        def head_prologue(s, h):
            S = SEQS[s]
            n = S // 128
            kt, bkt = kts.next()
            qt, bqt = qts.next()
            va, bva = vts.next()
            P.dma("sp", kt[:, 0:S], d["KT"][s][h, :, :], writes=[bkt])
            P.dma("sp", qt[0:64, 0, :], d["QT"][s][h, 0:64, :], writes=[bqt])
            P.dma("sp", qt[64:128, 1, :], d["QT"][s][h, 64:128, :], writes=[bqt])
            for k0 in range(0, n, 8):
                P.dma("sp", va[:, k0:k0 + 8, :],
                      d["VA"][s][k0 * 128:(k0 + 8) * 128, h * 130:(h + 1) * 130].rearrange("(kb p) c -> p kb c", p=128), writes=[bva],
                      group=True)
            bm, bbm = bms.next()
            P.dma("sp", bm[:], d["BM"][s][h], writes=[bbm])
            head[(s, h)] = (kt, bkt, qt, bqt, va, bva, bm, bbm)

        def st_qk(j):
            jb = jobs[j]
            s, h, t, jp, kb = jb["s"], jb["h"], jb["t"], jb["jp"], jb["kb"]
            if t == 0 and jb["first"]:
                head_prologue(s, h)
            kt, bkt, qt, bqt, va, bva, bmx, b_bmx = head[(s, h)]
            q0 = t * 512
            s12, bs12 = ps12.next()
            mm_raw(P, [(s12[:, 0:512], kt[:, kb * 128:(kb + 1) * 128], qt[:, 0, q0:q0 + 512], True, True),
                       (s12[:, 512:1024], kt[:, kb * 128:(kb + 1) * 128], qt[:, 1, q0:q0 + 512], True, True)],
                   [bkt, bqt], [bs12])
            jb["S"] = (s12, bs12)

        def st_exp(j):
            jb = jobs[j]
            s, h, t, jp, kb, n = jb["s"], jb["h"], jb["t"], jb["jp"], jb["kb"], jb["n"]
            s12, bs12 = jb["S"]
            bmx, b_bmx = head[(s, h)][6], head[(s, h)][7]
            p12, bp12 = p12s.next()
            if jp < 6:
                if t == 0 and jp == 0:
                    bt = bmx[:, 6, :]
                elif t == 3 and jp == 5:
                    bt = bmx[:, 7, :]
                else:
                    bt = bmx[:, 5 - jp, :]
                t12, bt12 = tm12.next()
                stt(P, t12[:, 0:512], s12[:, 0:512], 0.125, bt, ALU.mult, ALU.add, [bs12, b_bmx], [bt12])
                stt(P, t12[:, 512:1024], s12[:, 512:1024], 0.125, bt, ALU.mult, ALU.add, [bs12, b_bmx], [bt12])
                act(P, p12[:], t12[:], AF.Exp, [bt12], [bp12])
            else:
                col = t * n + kb if s == 0 else 128 + t * n + kb
                fb = K.farB[:, h, col:col + 1]
                act(P, p12[:], s12[:], AF.Exp, [bs12, bc], [bp12], scale=0.125, bias=fb)
            jb["P"] = (p12, bp12)

        def st_pv(j):
            jb = jobs[j]
            s, h, t, jp, kb, n = jb["s"], jb["h"], jb["t"], jb["jp"], jb["kb"], jb["n"]
            va, bva = head[(s, h)][4], head[(s, h)][5]
            p12, bp12 = jb["P"]
            items = []
            for m in range(2):
                for qb in range(4):
                    a = m * 4 + qb
                    items.append((pacc[a // 3][:, a % 3, :], p12[:, m * 512 + qb * 128:m * 512 + (qb + 1) * 128], va[:, kb, :],
                                  jb["first"] and a % 3 == 0, jb["last"]))
            mm_raw(P, items, [bp12, bva], [b_acc], skip=True)
            jb.pop("S", None)
            jb.pop("P", None)
            if jb["last"]:
                finalize(j, s, h, t)

        def finalize(j, s, h, t):
            ac, bac = accs.next()
            for b in range(3):
                nb = 3 if b < 2 else 2
                cp(P, ac[:, 3 * b:3 * b + nb, :], pacc[b][:, 0:nb, :], [b_acc], [bac])
            ys, bys = yst.next()
            tick0 = j + 2

            def part_a():
                st = []
                rd, brd = rds.next()
                P.op("dve", lambda e: e.reciprocal(out=rd[:], in_=ac[:, 0:8, 128]), [bac], [brd])
                for qb in range(4):
                    f, bf_ = fin.next()
                    of, bof = ofs.next()
                    o2, bo2 = of2.next()
                    ts(P, of[:], ac[:, qb, 0:128], rd[:, qb:qb + 1], None, ALU.mult, None, [bac, brd], [bof])
                    ts(P, o2[:], ac[:, 4 + qb, 0:128], rd[:, 4 + qb:5 + qb], K.neglam[:, 0:1], ALU.mult, ALU.mult, [bac, brd, bc], [bo2],
                       eng="dve")
                    tt(P, of[:], of[:], o2[:], ALU.add, [bo2, bof], [bof])
                    if qb == 0:
                        mset(P, f[:, 8:12], 0.0, [bf_])
                    st.append((f, bf_, of, bof))
                return st

            def part_b(st):
                f0, bf0 = st[0][0], st[0][1]
                for qb, (f, bf_, of, bof) in enumerate(st):
                    sq, bsq = osq.next()
                    act(P, sq[:], of[:], AF.Square, [bof], [bsq, bf0], accum_out=f0[:, 8 + qb:9 + qb])
                act(P, f0[:, 12:16], f0[:, 8:12], AF.Ln, [bf0, bc], [bf0], scale=1.0 / 128, bias=K.eps_c[:, 0:1])
                act(P, f0[:, 16:20], f0[:, 12:16], AF.Exp, [bf0], [bf0], scale=-0.5)

            def part_c(st):
                ons = []
                for qb, (f, bf_, of, bof) in enumerate(st):
                    f0, bf0 = st[0][0], st[0][1]
                    on, bon = onb.next()
                    stt(P, on[:], of[:], f0[:, 16 + qb:17 + qb], K.sg_bc[:], ALU.mult, ALU.mult, [bof, bf0, bc], [bon])
                    ons.append((on, bon))
                return ons

            def part_d(ons):
                for qb, (on, bon) in enumerate(ons):
                    pt, pb = ptr.next()
                    transposes(P, [(pt[:], on[:])], K.ident_b[:], [bon, bc], [pb])
                    cp(P, ys[:, qb * 128:(qb + 1) * 128], pt[:], [pb], [], pw=[bys])
                P.dma("pool", d["YA"][s][h, :, t * 512:(t + 1) * 512], ys[:], reads=[bys], writes=[K.b_scr], owner=bys)

            box = {}
            deferred.setdefault(tick0 + 2, []).append(lambda: box.__setitem__("st", part_a()))
            deferred.setdefault(tick0 + 5, []).append(lambda: part_b(box["st"]))
            deferred.setdefault(tick0 + 8, []).append(lambda: box.__setitem__("ons", part_c(box["st"])))
            deferred.setdefault(tick0 + 11, []).append(lambda: part_d(box["ons"]))

        run_pipeline(len(jobs), [st_qk, st_exp, st_pv], deferred)
        P.barrier()
        P.emit()


def phaseD(K):
    nc, P, d = K.nc, K.P, K.d
    bc = K.b_const
    G = d["G"]
    with ExitStack() as es:
        dq = es.enter_context(nc.sbuf_tensor("dq", [128, 4, OWN], BF16))
        dk = es.enter_context(nc.sbuf_tensor("dk", [128, 4, 4096], BF16))
        dv = es.enter_context(nc.sbuf_tensor("dv", [128, 32, 528], BF16))
        b_dq, b_dk, b_dv = P.buf("dq"), P.buf("dk"), P.buf("dv")
        tb = es.enter_context(nc.sbuf_tensor("tbd", [128, 8, 2, 128], F32))
        b_tb = P.buf("tbd")
        psAB = Slots(K, es, "psAB", [128, 2, 512], F32, 3, psum=True)
        pso = Slots(K, es, "pso", [128, 2, 66], F32, 2, psum=True)
        tmp = Slots(K, es, "tmpd", [128, 2, 256], F32, 2)
        pds = Slots(K, es, "pds", [128, 2, 256], BF16, 4)
        ost = Slots(K, es, "ostd", [128, 8, 66], F32, 2)
        jobs = []
        st_o = {}

        def group_prologue(s, g):
            r = DIL_R[g]
            nkt = 16 // r + 1
            P.dma("sp", dq[:], d["DQ"][s][g].rearrange("hp p t -> p hp t"), writes=[b_dq])
            P.dma("sp", dk[:], d["DK"][s][g].rearrange("hp p t -> p hp t"), writes=[b_dk])
            for c in range(r):
                row0 = 1024 + c - 64 * r
                src = bass.AP(d["DV"][s].tensor, (g * 4096 + row0) * 528, [[r * 528, 128], [128 * r * 528, nkt], [1, 528]])
                P.dma("sp", dv[:, c * nkt:(c + 1) * nkt, :], src, writes=[b_dv], group=True)
            P.dma("sp", tb[:], d["TB"][g], writes=[b_tb])

        def st0(i):
            jb = jobs[i]
            s, g, c, bi, hp = jb["s"], jb["g"], jb["c"], jb["bi"], jb["hp"]
            r = DIL_R[g]
            if c == 0 and bi == 0 and hp == 0:
                group_prologue(s, g)
            q_lo = c + r * 128 * bi
            kA = 1024 + c - 64 * r + 128 * r * bi
            kB = kA + 128 * r
            sAB, bsAB = psAB.next()
            items = []
            for e_ in range(2):
                pr = slice(64 * e_, 64 * e_ + 64)
                qa = dq[pr, hp, q_lo:q_lo + 127 * r + 1:r]
                items.append((sAB[:, e_, 0:128], dk[pr, hp, kB:kB + 127 * r + 1:r], qa, True, True))
                items.append((sAB[:, e_, 128:256], dk[pr, hp, kA:kA + 127 * r + 1:r], qa, True, True))
            mm_raw(P, items, [b_dq, b_dk], [bsAB])
            jb["S"] = (sAB, bsAB)

        def st1(i):
            jb = jobs[i]
            hp = jb["hp"]
            sAB, bsAB = jb["S"]
            t_, bt_ = tmp.next()
            stt(P, t_[:], sAB[:, :, 0:256], 0.125, tb[:, 2 * hp:2 * hp + 2, :, :].rearrange("p h a b -> p h (a b)"), ALU.mult, ALU.add,
                [bsAB, b_tb], [bt_])
            pd, bpd = pds.next()
            act(P, pd[:], t_[:], AF.Exp, [bt_], [bpd])
            jb["P"] = (pd, bpd)

        def st2(i):
            jb = jobs[i]
            s, g, c, bi, hp = jb["s"], jb["g"], jb["c"], jb["bi"], jb["hp"]
            r = DIL_R[g]
            nkt = 16 // r + 1
            if hp == 0:
                st_o["o"] = ost.next()
            o, bo = st_o["o"]
            po, bpo = pso.next()
            tA = c * nkt + bi
            pd, bpd = jb["P"]
            for e_ in range(2):
                hh = 2 * hp + e_
                mm(P, po[:, e_, :], [(pd[:, e_, 0:128], dv[:, tA + 1, hh * 66:(hh + 1) * 66]),
                                     (pd[:, e_, 128:256], dv[:, tA, hh * 66:(hh + 1) * 66])], [bpd, b_dv], [bpo])
            cp(P, o[:, 2 * hp:2 * hp + 2, :], po[:], [bpo], [], eng="dve", pw=[bo])
            if hp == 3:
                q_lo = c + r * 128 * bi
                dst = bass.AP(d["YB"][s].tensor, (g * OWN + q_lo) * 528, [[r * 528, 128], [1, 528]])
                P.dma("pool", dst, o[:].rearrange("p h e -> p (h e)"), reads=[bo], writes=[K.b_scr], owner=bo)

        for s in range(2):
            for g in range(3):
                r = DIL_R[g]
                jobs.clear()
                for c in range(r):
                    for bi in range(16 // r):
                        for hp in range(4):
                            jobs.append({"s": s, "g": g, "c": c, "bi": bi, "hp": hp})
                run_pipeline(len(jobs), [st0, None, st1, None, st2])
        P.barrier()
        P.emit()


def phaseE1(K):
    nc, P, d = K.nc, K.P, K.d
    bc = K.b_const
    with ExitStack() as es:
        wa = es.enter_context(nc.sbuf_tensor("wa", [128, 8, D], BF16))
        wbb = es.enter_context(nc.sbuf_tensor("wbb", [128, 4, D], BF16))
        wo = es.enter_context(nc.sbuf_tensor("wo", [128, 8, D], BF16))
        b_w = P.buf("we1")
        P.dma("sp", wa[:], d["WA"].rearrange("(c p) n -> p c n", p=128), reads=[K.b_wscr], writes=[b_w])
        P.dma("sp", wbb[:], d["WBB"].rearrange("(c p) n -> p c n", p=128), reads=[K.b_wscr], writes=[b_w])
        P.dma("sp", wo[:], d["WO"].rearrange("(c p) n -> p c n", p=128), reads=[K.b_wscr], writes=[b_w])
        yas = Slots(K, es, "yas", [128, 8, 128], BF16, 2)
        ybs = Slots(K, es, "ybs", [128, 3, 528], F32, 2)
        gts = Slots(K, es, "gts", [128, 2 * D], F32, 2)
        xs = Slots(K, es, "xe", [128, D], F32, 2)
        ybn = Slots(K, es, "ybn", [128, 512], BF16, 2)
        ybT = Slots(K, es, "ybT", [128, 4, 128], BF16, 2)
        rds = Slots(K, es, "rds", [128, 8], F32, 2)
        m1s = Slots(K, es, "m1s", [128, 512], F32, 2)
        m2s = Slots(K, es, "m2s", [128, 512], F32, 2)
        mg = Slots(K, es, "mg", [128, D], BF16, 2)
        mT = Slots(K, es, "mT", [128, 8, 128], BF16, 2)
        h2 = Slots(K, es, "h2", [128, 8, 512], BF16, 2)
        work = norm_work(K, es, "e")
        pa = Slots(K, es, "pae", [128, 512], F32, 2, psum=True)
        pb_ = Slots(K, es, "pbe", [128, 512], F32, 2, psum=True)
        pt4 = Slots(K, es, "pt4", [128, 8, 128], BF16, 2, psum=True)
        def tile_gen(s, c, j, ht, hb):
            t0 = c * 512 + j * 128
            ya, bya = yas.next()
            yb, byb = ybs.next()
            gt, bgt = gts.next()
            xt, bx = xs.next()
            P.dma("sp", ya[:], d["YA"][s][:, :, t0:t0 + 128].rearrange("h v t -> v h t"), writes=[bya])
            P.dma("sp", yb[:], d["YB"][s][:, t0:t0 + 128, :].rearrange("g t e -> t g e"), writes=[byb])
            P.dma("sp", gt[:], d["GT"][s][t0:t0 + 128, :], writes=[bgt])
            P.dma("sp", xt[:], d["xseq"][SBASE[s] + t0:SBASE[s] + t0 + 128, :], writes=[bx])
            tt(P, yb[:, 0, :], yb[:, 0, :], yb[:, 1, :], ALU.add, [byb], [byb])
            tt(P, yb[:, 0, :], yb[:, 0, :], yb[:, 2, :], ALU.add, [byb], [byb])
            rd, brd = rds.next()
            ybv = yb[:, 0, :].rearrange("p (h e) -> p h e", e=66)
            P.op("dve", lambda e, rd=rd, ybv=ybv: e.reciprocal(out=rd[:], in_=ybv[:, :, 64]), [byb], [brd])
            yield
            yn, byn = ybn.next()
            for hh in range(8):
                if hh % 2 == 0:
                    act(P, yn[:, hh * 64:(hh + 1) * 64], ybv[:, hh, 0:64], AF.Copy, [byb, brd], [], pw=[byn], scale=rd[:, hh:hh + 1])
                else:
                    ts(P, yn[:, hh * 64:(hh + 1) * 64], ybv[:, hh, 0:64], rd[:, hh:hh + 1], None, ALU.mult, None, [byb, brd], [], pw=[byn])
            yield
            p4, bp4 = pt4.next()
            transposes(P, [(p4[:, i, :], yn[:, i * 128:(i + 1) * 128]) for i in range(4)], K.ident_b[:], [byn, bc], [bp4])
            yield
            yt_, byt = ybT.next()
            cp(P, yt_[:], p4[:, 0:4, :], [bp4], [byt])
            yield
            mgt, bmg = mg.next()
            for half in range(2):
                cs = slice(half * 512, (half + 1) * 512)
                pA, bpA = pa.next()
                pB, bpB = pb_.next()
                mm(P, pA[:], [(ya[:, h, :], wa[:, h, cs]) for h in range(8)], [bya, b_w], [bpA])
                mm(P, pB[:], [(yt_[:, hp, :], wbb[:, hp, cs]) for hp in range(4)], [byt, b_w], [bpB])
                yield
                m1, bm1 = m1s.next()
                m2, bm2 = m2s.next()
                tt(P, m1[:], pA[:], gt[:, half * 512:(half + 1) * 512], ALU.mult, [bpA, bgt], [bm1])
                tt(P, m2[:], pB[:], gt[:, D + half * 512:D + (half + 1) * 512], ALU.mult, [bpB, bgt], [bm2])
                tt(P, mgt[:, cs], m1[:], m2[:], ALU.add, [bm1, bm2], [bmg])
                yield
            p8, bp8 = work["pT"].next()
            transposes(P, [(p8[:, kc, :], mgt[:, kc * 128:(kc + 1) * 128]) for kc in range(8)], K.ident_b[:], [bmg, bc], [bp8])
            yield
            mt, bmt = mT.next()
            cp(P, mt[:], p8[:], [bp8], [bmt])
            yield
            for half in range(2):
                cs = slice(half * 512, (half + 1) * 512)
                pA, bpA = pa.next()
                mm(P, pA[:], [(mt[:, kc, :], wo[:, kc, cs]) for kc in range(8)], [bmt, b_w], [bpA])
                yield
                m1, bm1 = m1s.next()
                tt(P, m1[:], pA[:], K.gate_bc[:, s, 0, cs], ALU.mult, [bpA, bc], [bm1])
                tt(P, xt[:, cs], xt[:, cs], m1[:], ALU.add, [bx, bm1], [bx])
                yield
            P.dma("pool", d["X1"][s][t0:t0 + 128, :], xt[:], reads=[bx], writes=[K.b_scr], owner=bx)
            sq, bsq = work["sq"].next()
            st, bst = work["st"].next()
            xn, bxn = work["xn"].next()
            mset(P, st[:, 0:1], 0.0, [bst])
            act(P, sq[:], xt[:], AF.Square, [bx], [bsq, bst], accum_out=st[:, 0:1])
            act(P, st[:, 1:2], st[:, 0:1], AF.Ln, [bst, bc], [bst], scale=1.0 / D, bias=K.eps_c[:, 0:1])
            act(P, st[:, 2:3], st[:, 1:2], AF.Exp, [bst], [bst], scale=-0.5)
            ts(P, xn[:], xt[:], st[:, 2:3], None, ALU.mult, None, [bx, bst], [bxn])
            yield
            pT, bpT = work["pT"].next()
            transposes(P, [(pT[:, kc, :], xn[:, kc * 128:(kc + 1) * 128]) for kc in range(8)], K.ident_b[:], [bxn, bc], [bpT])
            yield
            for kc in range(8):
                o = ht[:, kc, j * 128:(j + 1) * 128]
                if j % 2 == 0:
                    act(P, o, pT[:, kc, :], AF.Identity, [bpT, bc], [], pw=[hb], scale=K.A2[:, s, kc:kc + 1], bias=K.B2[:, s, kc:kc + 1])
                else:
                    ts(P, o, pT[:, kc, :], K.A2[:, s, kc:kc + 1], K.B2[:, s, kc:kc + 1], ALU.mult, ALU.add, [bpT, bc], [], pw=[hb])

        for s in range(2):
            for c in range(4):
                ht, hb = h2.next()
                for j0 in (0, 2):
                    gens = [tile_gen(s, c, j0, ht, hb), tile_gen(s, c, j0 + 1, ht, hb)]
                    while gens:
                        for g_ in list(gens):
                            try:
                                next(g_)
                            except StopIteration:
                                gens.remove(g_)
                P.dma("pool", d["H2"][s][:, :, c * 512:(c + 1) * 512], ht[:], reads=[hb], writes=[K.b_scr], owner=hb)
        P.barrier()
        P.emit()


def phaseE1b(K):
    nc, P, d = K.nc, K.P, K.d
    with ExitStack() as es:
        wup = es.enter_context(nc.sbuf_tensor("wup", [128, 8, 4 * D], BF16))
        b_w = P.buf("we1b")
        for i in range(4):
            P.dma("sp", wup[:, :, i * D:(i + 1) * D], d["WUP"][:, i * D:(i + 1) * D].rearrange("(c p) n -> p c n", p=128), reads=[K.b_wscr], writes=[b_w], group=True)
        h2 = Slots(K, es, "h2b", [128, 8, 512], BF16, 2)
        rl = Slots(K, es, "rl", [128, 512], F32, 2)
        us = Slots(K, es, "us", [128, 512], BF16, 6)
        pa = Slots(K, es, "pau", [128, 512], F32, 4, psum=True)
        for s in range(2):
            for c in range(4):
                ht, hb = h2.next()
                P.dma("sp", ht[:], d["H2"][s][:, :, c * 512:(c + 1) * 512], writes=[hb])
                for f in range(32):
                    pA, bpA = pa.next()
                    mm(P, pA[:], [(wup[:, kc, f * 128:(f + 1) * 128], ht[:, kc, :]) for kc in range(8)], [b_w, hb], [bpA])
                    r_, br = rl.next()
                    act(P, r_[:], pA[:], AF.Relu, [bpA], [br])
                    u, bu = us.next()
                    tt(P, u[:], r_[:], r_[:], ALU.mult, [br], [bu])
                    P.dma("pool", d["UT"][s][f, :, c * 512:(c + 1) * 512], u[:], reads=[bu], writes=[K.b_scr], owner=bu)
        P.barrier()
        P.emit()


def phaseE2(K):
    nc, P, d = K.nc, K.P, K.d
    bc = K.b_const
    with ExitStack() as es:
        wd = es.enter_context(nc.sbuf_tensor("wd", [128, 32, D], BF16))
        b_w = P.buf("we2")
        for i in range(4):
            P.dma("sp", wd[:, i * 8:(i + 1) * 8, :], d["WDN"][i * D:(i + 1) * D, :].rearrange("(c p) n -> p c n", p=128), reads=[K.b_wscr], writes=[b_w], group=True)
        uts = Slots(K, es, "uts", [128, 32, 256], BF16, 2)
        x1s = Slots(K, es, "x1s", [128, D], F32, 2)
        m1s = Slots(K, es, "m1z", [128, 512], F32, 2)
        pa = Slots(K, es, "paz", [128, 512], F32, 4, psum=True)
        for s in range(2):
            for c in range(8):
                ut, but = uts.next()
                for i in range(4):
                    P.dma("sp", ut[:, i * 8:(i + 1) * 8, :], d["UT"][s][i * 8:(i + 1) * 8, :, c * 256:(c + 1) * 256].rearrange("f p t -> p f t"),
                          writes=[but], group=True)
                for j in range(2):
                    t0 = c * 256 + j * 128
                    xt, bx = x1s.next()
                    P.dma("sp", xt[:], d["X1"][s][t0:t0 + 128, :], writes=[bx])
                    for half in range(2):
                        cs = slice(half * 512, (half + 1) * 512)
                        pA, bpA = pa.next()
                        mm(P, pA[:], [(ut[:, f, j * 128:(j + 1) * 128], wd[:, f, cs]) for f in range(32)], [but, b_w], [bpA])
                        m1, bm1 = m1s.next()
                        tt(P, m1[:], pA[:], K.gate_bc[:, s, 1, cs], ALU.mult, [bpA, bc], [bm1])
                        tt(P, xt[:, cs], xt[:, cs], m1[:], ALU.add, [bx, bm1], [bx])
                    P.dma("pool", d["y"][s * OWN + t0:s * OWN + t0 + 128, :], xt[:], reads=[bx], writes=[K.b_out], owner=bx)
        P.barrier()
        P.emit()


WEIGHTS = [("rel_bias", [32, 32]), ("norm1_g", [1, D]), ("w_ada", [D, 6 * D]), ("b_ada", [1, 6 * D]), ("w_in", [D, 9728]),
           ("qn_a", [1, 64]), ("kn_a", [1, 64]), ("lambda_q1", [1, 64]), ("lambda_k1", [1, 64]), ("lambda_q2", [1, 64]),
           ("lambda_k2", [1, 64]), ("subln_g", [1, 128]), ("qn_b", [3, 64]), ("kn_b", [3, 64]), ("w_br_a", [D, D]),
           ("w_br_b", [512, D]), ("w_o", [D, D]), ("norm2_g", [1, D]), ("w_up", [D, 4 * D]), ("w_down", [4 * D, D])]

SCRATCH = {
    "G": ([32, OHW], F32),
}


def build(phases=("0", "A", "B", "C", "D", "E1", "E1b", "E2"), debug=False):
    nc = bass.Bass("TRN2", target_bir_lowering=False)
    K = Ctx()
    K.nc = nc
    P = Prog(nc)
    K.P = P
    d = {}
    inp = lambda name, shape: nc.dram_tensor(name, shape, F32, kind="ExternalInput").ap()
    d["xseq"] = inp("xseq", [12288, D])
    d["c2"] = inp("c2", [2, D])
    d["oh"] = inp("oh", [33, OHW])
    d["ohfar"] = inp("ohfar", [32, 384])
    d["valid"] = inp("valid", [2, 4096])
    for nm, shp in WEIGHTS:
        d[nm] = inp(nm, shp)
    d["y"] = nc.dram_tensor("y", [2 * OWN, D], F32, kind="ExternalOutput").ap()

    def scr(name, shape, dt):
        if debug:
            return nc.dram_tensor(name, shape, dt, kind="ExternalOutput").ap()
        return nc.dram_tensor(name, shape, dt).ap()

    d["G"] = scr("G", [32, OHW], F32)
    d["WB"] = nc.dram_tensor("WB", [D, 9728], BF16).ap()
    d["WA"] = nc.dram_tensor("WA", [D, D], BF16).ap()
    d["WBB"] = nc.dram_tensor("WBB", [512, D], BF16).ap()
    d["WO"] = nc.dram_tensor("WO", [D, D], BF16).ap()
    d["WUP"] = nc.dram_tensor("WUP", [D, 4 * D], BF16).ap()
    d["WDN"] = nc.dram_tensor("WDN", [4 * D, D], BF16).ap()
    d["BM"] = [scr(f"BM{s_}", [8, 128, 8, 512], F32) for s_ in range(2)]
    d["TB"] = scr("TB", [3, 128, 8, 2, 128], F32)
    for k_ in ("hT", "KT", "QT", "VA", "DQ", "DK", "DV", "GT", "YA", "YB", "X1", "UT", "H2"):
        d[k_] = [None, None]
    for s in range(2):
        S = SEQS[s]
        d["hT"][s] = scr(f"hT{s}", [128, 8, S], BF16)
        d["KT"][s] = scr(f"KT{s}", [8, 128, S], BF16)
        d["QT"][s] = scr(f"QT{s}", [8, 128, OWN], BF16)
        d["VA"][s] = scr(f"VA{s}", [S, 8 * 130], BF16)
        d["DQ"][s] = scr(f"DQ{s}", [3, 4, 128, OWN], BF16)
        d["DK"][s] = scr(f"DK{s}", [3, 4, 128, 4096], BF16)
        d["DV"][s] = scr(f"DV{s}", [3, 4096, 528], BF16)
        d["GT"][s] = scr(f"GT{s}", [OWN, 2 * D], F32)
        d["YA"][s] = scr(f"YA{s}", [8, 128, OWN], BF16)
        d["YB"][s] = scr(f"YB{s}", [3, OWN, 528], F32)
        d["X1"][s] = scr(f"X1{s}", [OWN, D], F32)
        d["UT"][s] = scr(f"UT{s}", [32, 128, OWN], BF16)
        d["H2"][s] = scr(f"H2{s}", [128, 8, OWN], BF16)
    K.d = d
    K.b_const = P.buf("const", multi=True)
    K.b_scr = P.buf("scr", multi=True)
    K.b_wscr = P.buf("wscr", multi=True)
    K.b_out = P.buf("out", multi=True)
    sbt = lambda name, shape, dt=F32: nc.alloc_sbuf_tensor(name, shape, dt)
    K.ident_f = sbt("ident_f", [128, 128])
    K.ident_b = sbt("ident_b", [128, 128], BF16)
    K.J_f = sbt("J_f", [128, 128])
    K.ones_f = sbt("ones_f", [128, 128])
    K.blk_f = sbt("blk_f", [128, 128])
    K.blk_b = sbt("blk_b", [128, 128], BF16)
    K.eps_c = sbt("eps_c", [128, 1])
    K.A1 = sbt("A1", [128, 2, 8])
    K.B1 = sbt("B1", [128, 2, 8])
    K.A2 = sbt("A2", [128, 2, 8])
    K.B2 = sbt("B2", [128, 2, 8])
    K.gate_bc = sbt("gate_bc", [128, 2, 2, D])
    K.gains = sbt("gains", [128, 8])
    K.sg_bc = sbt("sg_bc", [128, 128])
    K.neglam = sbt("neglam", [128, 1])
    K.valid_c = sbt("valid_c", [128, 2, 32])
    K.farB = sbt("farB", [128, 8, 384])
    fns = {"0": phase0, "A": phaseA, "B": phaseB, "C": phaseC, "D": phaseD, "E1": phaseE1, "E1b": phaseE1b, "E2": phaseE2}
    for ph in phases:
        fns[ph](K)
    return nc, K


def _rel_bucket(rel):
    nb, max_exact = 16, 8
    rel = np.asarray(rel, dtype=np.int64)
    ret = np.where(rel > 0, nb, 0)
    n = np.abs(rel)
    nf = np.maximum(n, 1).astype(np.float32)
    large = max_exact + (np.log(nf / np.float32(max_exact)) / np.float32(math.log(128 / max_exact))
                         * np.float32(nb - max_exact)).astype(np.int32)
    large = np.minimum(large, nb - 1)
    return (ret + np.where(n < max_exact, n, large)).astype(np.int64)


def _core_tables(core):
    p0 = (2048 * (core % 2), 2048 * (core % 4))
    oh = np.zeros((33, OHW), np.float32)
    m = np.arange(GW)
    bmain = _rel_bucket(639 - m)
    oh[bmain, m] = 1.0
    for s in range(2):
        S = SEQS[s]
        left_edge = p0[s] == 0
        right_edge = p0[s] + OWN == S
        be0 = np.full(GW, 31) if left_edge else bmain
        be5 = np.full(GW, 15) if right_edge else bmain
        oh[be0, (1 + 2 * s) * GW + m] = 1.0
        oh[be5, (2 + 2 * s) * GW + m] = 1.0
    mm_ = np.arange(DW)
    jj = 191 - mm_
    for g, r in enumerate(DIL_R):
        b = _rel_bucket(jj * r)
        b = np.where(np.abs(jj) <= 64, b, 32)
        oh[b, 5 * GW + g * DW + mm_] = 1.0
    ohfar = np.zeros((32, 384), np.float32)
    col = 0
    for s in range(2):
        n = SEQS[s] // 128
        for t in range(4):
            for kb in range(n):
                gk = (p0[s] // 128 + kb) % n
                gq = p0[s] // 128 + 4 * t
                ohfar[15 if gk < gq else 31, col] = 1.0
                col += 1
    valid = np.zeros((2, 4096), np.float32)
    for s in range(2):
        gpos = p0[s] - 1024 + np.arange(4096)
        valid[s] = ((gpos >= 0) & (gpos < SEQS[s])).astype(np.float32)
    return oh, ohfar, valid


def make_in_maps(inputs):
    w = {}
    for nm, shp in WEIGHTS:
        a = np.asarray(inputs[nm], dtype=np.float32)
        if nm != "rel_bias":
            a = a[0]
        w[nm] = np.ascontiguousarray(a.reshape(shp))
    xp = np.asarray(inputs["x_prompt"], dtype=np.float32)
    xs = np.asarray(inputs["x_sample"], dtype=np.float32)
    cp_ = np.asarray(inputs["c_prompt"], dtype=np.float32)
    cs = np.asarray(inputs["c_sample"], dtype=np.float32)
    maps = []
    for core in range(8):
        pb, ph = core // 2, core % 2
        sb_, sq = core // 4, core % 4
        xseq = np.concatenate([np.roll(xp[pb], -2048 * ph, axis=0), np.roll(xs[sb_], -2048 * sq, axis=0)], axis=0)
        oh, ohfar, valid = _core_tables(core)
        m = {"xseq": np.ascontiguousarray(xseq), "c2": np.ascontiguousarray(np.stack([cp_[pb], cs[sb_]])),
             "oh": oh, "ohfar": ohfar, "valid": valid}
        m.update(w)
        maps.append(m)
    return maps


_NC = None


def kernel(**inputs):
    global _NC
    if _NC is None:
        _NC = build()[0]
    maps = make_in_maps(inputs)
    res = run_bass_kernel_spmd(_NC, maps, core_ids=list(range(8)))
    yp = np.zeros((4, 4096, D), np.float32)
    ys = np.zeros((2, 8192, D), np.float32)
    for core in range(8):
        y = res.results[core]["y"]
        pb, ph = core // 2, core % 2
        sb_, sq = core // 4, core % 4
        yp[pb, 2048 * ph:2048 * (ph + 1)] = y[0:OWN]
        ys[sb_, 2048 * sq:2048 * (sq + 1)] = y[OWN:2 * OWN]
    return (yp, ys)
```

```python
import math
from contextlib import ExitStack
import numpy as np
import concourse.bass as bass
import concourse.mybir as mybir
from concourse.bass_utils import run_bass_kernel_spmd

F32 = mybir.dt.float32
BF16 = mybir.dt.bfloat16
AF = mybir.ActivationFunctionType
ALU = mybir.AluOpType
AX = mybir.AxisListType

D = 1024
SEQS = (4096, 8192)
SBASE = (0, 4096)
OWN = 2048
NEG = -30000.0
EPS = 1e-6
GW = 1280
DW = 384
OHW = 5 * GW + 3 * DW
DIL_R = (1, 4, 16)
SAME_ENGINE_SYNC = True
DBG = {}


class Buf:
    __slots__ = ("name", "w", "r", "ds", "multi", "wm", "pw")

    def __init__(self, name, multi=False):
        self.name = name
        self.w = None
        self.r = []
        self.ds = {}
        self.multi = multi
        self.wm = {}
        self.pw = []


class Prog:
    CE = ("pe", "act", "dve", "pool")
    ALLQ = ("pe", "act", "dve", "pool", "sp")
    ENGMAP = {"pe": "tensor", "act": "scalar", "dve": "vector", "pool": "gpsimd", "sp": "sync"}

    def __init__(self, nc):
        self.nc = nc
        self.q = {e: [] for e in self.ALLQ}
        self.sem = {e: nc.alloc_semaphore(name=f"pg_{e}") for e in self.CE}
        self.cnt = {e: 0 for e in self.CE}
        self.seen = {e: {} for e in self.ALLQ}
        self.dma_bufs = []
        self.pool = {"hw": [], "sw": []}
        self.final = {}
        self.nsem = 0
        self.ninst = 0

    def buf(self, name, multi=False):
        return Buf(name, multi)

    def _deps(self, eng, reads, writes, pwrites=()):
        deps = []
        for b in reads:
            if b.multi:
                deps.extend(b.wm.values())
            else:
                if b.w is not None:
                    deps.append(b.w)
                deps.extend(b.pw)
        for b in pwrites:
            if b.w is not None:
                deps.append(b.w)
            deps.extend(b.r)
        for b in writes:
            if b.multi:
                continue
            if b.w is not None:
                deps.append(b.w)
            deps.extend(b.pw)
            deps.extend(b.r)
        best = {}
        for (sem, val, te) in deps:
            if te == eng and not SAME_ENGINE_SYNC:
                continue
            if te == "pe" and eng == "pe":
                continue
            k = id(sem)
            if k not in best or best[k][1] < val:
                best[k] = (sem, val)
        waits = []
        seen = self.seen[eng]
        for k, (sem, val) in best.items():
            if seen.get(k, 0) >= val:
                continue
            seen[k] = val
            waits.append((sem, val))
        return waits

    def _commit(self, tok, reads, writes, pwrites=()):
        for b in reads:
            if not b.multi:
                b.r.append(tok)
        for b in pwrites:
            b.pw.append(tok)
            if len(b.pw) > 64:
                b.pw = b.pw[-64:]
            if len(b.r) > 64:
                b.r = b.r[-64:]
        for b in writes:
            if b.multi:
                b.wm[id(tok[0])] = tok
            else:
                b.w = tok
                b.r = []
                b.pw = []

    def op(self, eng, fn, reads=(), writes=(), pwrites=()):
        waits = self._deps(eng, reads, writes, pwrites)
        if self.cnt[eng] >= 30000:
            self.final[id(self.sem[eng])] = (self.sem[eng], self.cnt[eng])
            self.sem[eng] = self.nc.alloc_semaphore(name=f"pg_{eng}_{self.nsem}")
            self.nsem += 1
            self.cnt[eng] = 0
        self.cnt[eng] += 1
        tok = (self.sem[eng], self.cnt[eng], eng)
        self.q[eng].append((waits, fn, (self.sem[eng], 1)))
        self._commit(tok, reads, writes, pwrites)
        return tok

    def dma(self, eng, out, in_, reads=(), writes=(), owner=None, group=False, **kw):
        ob = owner if owner is not None else writes[0]
        kind = "sw" if eng == "pool" else "hw"
        saved = None
        if group and kind in ob.ds and len(writes) == 1 and writes[0].w is not None \
                and writes[0].w[0] is ob.ds[kind][0] and not writes[0].r:
            saved = writes[0].w
            writes[0].w = None
        waits = self._deps(eng, reads, writes)
        if saved is not None:
            writes[0].w = saved
        if kind not in ob.ds:
            if self.pool[kind]:
                ob.ds[kind] = list(self.pool[kind].pop())
            else:
                ob.ds[kind] = [self.nc.alloc_semaphore(name=f"d{self.nsem}"), 0]
                self.nsem += 1
            self.dma_bufs.append((ob, kind))
        ent = ob.ds[kind]
        ent[1] += 16
        tok = (ent[0], ent[1], None)
        self.final[id(ent[0])] = (ent[0], ent[1])

        def fn(e, out=out, in_=in_, kw=kw):
            return e.dma_start(out=out, in_=in_, **kw)

        self.q[eng].append((waits, fn, (ent[0], 16)))
        self._commit(tok, reads, writes)
        return tok

    def barrier(self, include_bg=False):
        bg_ids = set()
        if not include_bg:
            for (b, kind) in self.dma_bufs:
                if b.name.startswith("cv"):
                    bg_ids.add(id(b.ds[kind][0]))
        toks = [(self.sem[e], self.cnt[e]) for e in self.CE if self.cnt[e]]
        toks += [v for k_, v in self.final.items() if k_ not in bg_ids]
        for e in self.ALLQ:
            seen = self.seen[e]
            waits = []
            for (sem, val) in toks:
                if seen.get(id(sem), 0) >= val:
                    continue
                seen[id(sem)] = val
                waits.append((sem, val))
            if waits:
                self.q[e].append((waits, None, None))
        keep = []
        for (b, kind) in self.dma_bufs:
            if id(b.ds[kind][0]) in bg_ids:
                keep.append((b, kind))
            else:
                self.pool[kind].append(tuple(b.ds.pop(kind)))
        self.dma_bufs = keep

    def emit(self):
        nc = self.nc
        with nc.Block() as block:
            for e in self.ALLQ:
                items = self.q[e]
                if not items:
                    continue

                def body(engine, items=items):
                    for (waits, fn, inc) in items:
                        for (sem, val) in waits:
                            engine.wait_ge(sem, val)
                        if fn is not None:
                            fn(engine).then_inc(inc[0], inc[1])

                getattr(block, self.ENGMAP[e])(body)
                self.ninst += len(items)
        self.q = {e: [] for e in self.ALLQ}


def mm(P, out, pairs, reads, writes):
    def fn(e):
        n = len(pairs)
        ins = None
        for i, (l, r) in enumerate(pairs):
            ins = e.matmul(out, lhsT=l, rhs=r, start=(i == 0), stop=(i == n - 1))
        return ins
    P.op("pe", fn, reads, writes)


def mm_multi(P, groups, reads, writes):
    def fn(e):
        ins = None
        for (out, pairs) in groups:
            n = len(pairs)
            for i, (l, r) in enumerate(pairs):
                ins = e.matmul(out, lhsT=l, rhs=r, start=(i == 0), stop=(i == n - 1))
        return ins
    P.op("pe", fn, reads, writes)


def mm_raw(P, items, reads, writes, skip=False):
    def fn(e):
        ins = None
        for (o, l, r, st, sp) in items:
            if skip:
                ins = e.matmul(o, lhsT=l, rhs=r, start=st, stop=sp, skip_group_check=True)
            else:
                ins = e.matmul(o, lhsT=l, rhs=r, start=st, stop=sp)
        return ins
    P.op("pe", fn, reads, writes)


def transposes(P, items, ident, reads, writes):
    def fn(e):
        ins = None
        for (o, i) in items:
            ins = e.transpose(out=o, in_=i, identity=ident)
        return ins
    P.op("pe", fn, reads, writes)


def act(P, out, in_, func, reads, writes, pw=(), **kw):
    P.op("act", lambda e: e.activation(out=out, in_=in_, func=func, **kw), reads, writes, pw)


def ts(P, out, in0, s1, s2, op0, op1, reads, writes, eng="dve", pw=()):
    if s2 is None:
        P.op(eng, lambda e: e.tensor_scalar(out=out, in0=in0, scalar1=s1, scalar2=None, op0=op0), reads, writes, pw)
    else:
        P.op(eng, lambda e: e.tensor_scalar(out=out, in0=in0, scalar1=s1, scalar2=s2, op0=op0, op1=op1), reads, writes, pw)


def tt(P, out, in0, in1, op, reads, writes, eng="dve"):
    P.op(eng, lambda e: e.tensor_tensor(out=out, in0=in0, in1=in1, op=op), reads, writes)


def stt(P, out, in0, scalar, in1, op0, op1, reads, writes, eng="dve"):
    P.op(eng, lambda e: e.scalar_tensor_tensor(out=out, in0=in0, scalar=scalar, in1=in1, op0=op0, op1=op1), reads, writes)


def cp(P, out, in_, reads, writes, eng="dve", pw=()):
    P.op(eng, lambda e: e.tensor_copy(out=out, in_=in_), reads, writes, pw)


def mset(P, ap, val, writes, eng="dve"):
    P.op(eng, lambda e: e.memset(ap, val), (), writes)


class Ctx:
    pass


class _View:
    def __init__(self, ap):
        self.ap = ap

    def __getitem__(self, idx):
        return self.ap[idx]


class Slots:
    def __init__(self, K, es, name, shape, dt, n, psum=False):
        self.t = []
        self.b = []
        for i in range(n):
            if psum:
                nbytes = int(np.prod(shape[1:])) * (4 if dt == F32 else 2)
                assert nbytes <= 2048 or nbytes % 2048 == 0, (name, shape)
                if nbytes < 2048:
                    full = es.enter_context(K.nc.psum_tensor(f"{name}{i}", [128, 512 if dt == F32 else 1024], dt))
                    n_el = int(np.prod(shape[1:]))
                    t = full[:, 0:n_el]
                    if len(shape) == 3:
                        t = t.rearrange("p (a b) -> p a b", b=shape[2])
                    elif len(shape) == 4:
                        t = t.rearrange("p (a b c) -> p a b c", b=shape[2], c=shape[3])
                    t = _View(t)
                else:
                    t = es.enter_context(K.nc.psum_tensor(f"{name}{i}", shape, dt))
            else:
                t = es.enter_context(K.nc.sbuf_tensor(f"{name}{i}", shape, dt))
            self.t.append(t)
            self.b.append(K.P.buf(f"{name}{i}"))
        self.i = 0
        self.n = n

    def next(self):
        j = self.i % self.n
        self.i += 1
        return self.t[j], self.b[j]


def rstd_ops(K, ss, tmp, out, scale, reads, b_tmp, b_out):
    act(K.P, tmp, ss, AF.Ln, reads + [K.b_const], [b_tmp], scale=scale, bias=K.eps_c[:, 0:1])
    act(K.P, out, tmp, AF.Exp, [b_tmp], [b_out], scale=-0.5)


def phase0(K):
    nc, P = K.nc, K.P
    d = K.d
    bc = K.b_const
    with ExitStack() as es:
        sb = lambda name, shape, dt=F32: es.enter_context(nc.sbuf_tensor("s0_" + name, shape, dt))
        cv = [0]

        def conv(dst, src):
            cv[0] += 1
            P.dma("pool", dst, src, writes=[K.b_wscr], owner=P.buf(f"cv{cv[0]}"))
        w_in = d["w_in"]
        WB = d["WB"]
        for rb in range(4):
            rs = slice(rb * 256, (rb + 1) * 256)
            for (dst0, cA, cB) in ((0, 0, 512), (1024, 1024, 1536)):
                dv_ = WB[rs, dst0:dst0 + 1024].rearrange("p (h e) -> p h e", e=128)
                conv(dv_[:, :, 0:64], w_in[rs, cA:cA + 512].rearrange("p (h e) -> p h e", e=64))
                conv(dv_[:, :, 64:128], w_in[rs, cB:cB + 512].rearrange("p (h e) -> p h e", e=64))
            conv(WB[rs, 2048:9728], w_in[rs, 2048:9728])
        conv(d["WA"][:, :], d["w_br_a"][:, :])
        conv(d["WBB"][:, :], d["w_br_b"][:, :])
        conv(d["WO"][:, :], d["w_o"][:, :])
        for rb in range(4):
            conv(d["WUP"][rb * 256:(rb + 1) * 256, :], d["w_up"][rb * 256:(rb + 1) * 256, :])
            conv(d["WDN"][rb * 1024:(rb + 1) * 1024, :], d["w_down"][rb * 1024:(rb + 1) * 1024, :])
        b_idf, b_jf, b_blk = P.buf("idf"), P.buf("jf"), P.buf("blk")
        mset(P, K.ident_f[:], 0.0, [b_idf], eng="pool")
        P.op("pool", lambda e: e.affine_select(out=K.ident_f[:], in_=K.ident_f[:], pattern=[[-1, 128]], compare_op=ALU.not_equal,
                                               fill=1.0, base=0, channel_multiplier=1), [b_idf], [b_idf])
        mset(P, K.J_f[:], 0.0, [b_jf], eng="pool")
        P.op("pool", lambda e: e.affine_select(out=K.J_f[:], in_=K.J_f[:], pattern=[[1, 128]], compare_op=ALU.not_equal,
                                               fill=1.0, base=-127, channel_multiplier=1), [b_jf], [b_jf])
        mset(P, K.ones_f[:], 1.0, [bc])
        mset(P, K.eps_c[:], EPS, [bc])
        mset(P, K.blk_f[:], 0.0, [b_blk])
        mset(P, K.blk_f[0:64, 0:64], 1.0, [b_blk])
        mset(P, K.blk_f[64:128, 64:128], 1.0, [b_blk])
        relb = sb("relb", [33, 32])
        oh = sb("oh", [33, OHW])
        ohfar = sb("ohfar", [32, 384])
        g1 = sb("g1", [128, 8])
        g2 = sb("g2", [128, 8])
        bada_pp = sb("bada_pp", [128, 48])
        bada_row = sb("bada_row", [1, 6 * D])
        cpp = sb("cpp", [128, 2, 8])
        lam4 = sb("lam4", [128, 4, 64])
        sgt = sb("sgt", [128, 128])
        ld = lambda out, in_, **kw: P.dma("sp", out, in_, writes=[bc], **kw)
        mset(P, relb[32:33, :], NEG, [bc])
        ld(relb[0:32, :], d["rel_bias"][:, :])
        ld(oh[:], d["oh"][:, :])
        ld(ohfar[:], d["ohfar"][:, :])
        ld(g1[:], d["norm1_g"].rearrange("o (c p) -> p (o c)", p=128), allow_slow_non_contiguous=True)
        ld(g2[:], d["norm2_g"].rearrange("o (c p) -> p (o c)", p=128), allow_slow_non_contiguous=True)
        ld(bada_pp[:], d["b_ada"].rearrange("o (j p) -> p (o j)", p=128), allow_slow_non_contiguous=True)
        ld(bada_row[:], d["b_ada"][:, :])
        ld(cpp[:], d["c2"].rearrange("s (c p) -> p s c", p=128), allow_slow_non_contiguous=True)
        for i, nm in enumerate(("lambda_q1", "lambda_k1", "lambda_q2", "lambda_k2")):
            ld(lam4[:, i, :], bass.AP(d[nm].tensor, 0, [[0, 128], [1, 64]]))
        ld(sgt[:], bass.AP(d["subln_g"].tensor, 0, [[0, 128], [1, 128]]))
        gsrc = [("qn_a", 0), ("kn_a", 0), ("qn_b", 0), ("kn_b", 0), ("qn_b", 1), ("kn_b", 1), ("qn_b", 2), ("kn_b", 2)]
        for i, (nm, row) in enumerate(gsrc):
            src = bass.AP(d[nm].tensor, row * 64, [[1, 64], [1, 1]])
            ld(K.gains[0:64, i:i + 1], src)
            ld(K.gains[64:128, i:i + 1], src)
        ld(K.valid_c[:], d["valid"].rearrange("s (n p) -> p s n", p=128), allow_slow_non_contiguous=True)
        P.barrier()
        cp(P, K.ident_b[:], K.ident_f[:], [b_idf], [bc])
        cp(P, K.blk_b[:], K.blk_f[:], [b_blk], [bc])
        ts(P, K.sg_bc[:], sgt[:], 0.8, None, ALU.mult, None, [bc], [bc])
        lp = sb("lp", [128, 2, 64])
        ls = sb("ls", [128, 2])
        le = sb("le", [128, 2])
        tt(P, lp[:, 0, :], lam4[:, 0, :], lam4[:, 1, :], ALU.mult, [bc], [bc])
        tt(P, lp[:, 1, :], lam4[:, 2, :], lam4[:, 3, :], ALU.mult, [bc], [bc])
        P.op("dve", lambda e: e.reduce_sum(out=ls[:, 0:1], in_=lp[:, 0, :], axis=AX.X), [bc], [bc])
        P.op("dve", lambda e: e.reduce_sum(out=ls[:, 1:2], in_=lp[:, 1, :], axis=AX.X), [bc], [bc])
        act(P, le[:], ls[:], AF.Exp, [bc], [bc])
        tt(P, K.neglam[:], le[:, 1:2], le[:, 0:1], ALU.subtract, [bc], [bc])
        ts(P, K.neglam[:], K.neglam[:], -0.2, None, ALU.add, None, [bc], [bc])
        with ExitStack() as es2:
            gsb = es2.enter_context(nc.sbuf_tensor("gsb", [32, OHW], F32))
            pg = Slots(K, es2, "pg", [128, 512], F32, 2, psum=True)
            c0 = 0
            while c0 < OHW:
                w = min(512, OHW - c0)
                pt, pb = pg.next()
                mm(P, pt[0:32, 0:w], [(relb[0:33, :], oh[0:33, c0:c0 + w])], [bc], [pb])
                cp(P, gsb[:, c0:c0 + w], pt[0:32, 0:w], [pb], [bc])
                c0 += w
            P.dma("sp", d["G"][:, :], gsb[:], reads=[bc], writes=[K.b_scr], owner=bc)
            rbs = Slots(K, es2, "rbh", [32, 128], F32, 2)
            for h in range(8):
                rb, brb = rbs.next()
                ts(P, rb[:], K.ones_f[0:32, :], relb[0:32, h:h + 1], None, ALU.mult, None, [bc], [brb])
                pt, pb = pg.next()
                mm(P, pt[:, 0:384], [(rb[:], ohfar[:])], [bc, brb], [pb])
                cp(P, K.farB[:, h, :], pt[:, 0:384], [pb], [bc])
            P.barrier()
            P.emit()
        with ExitStack() as es2:
            sc = es2.enter_context(nc.sbuf_tensor("sc", [128, 2, 8], F32))
            scb = es2.enter_context(nc.sbuf_tensor("scb", [128, 2, 8, 128], F32))
            modpp = es2.enter_context(nc.sbuf_tensor("modpp", [128, 4, 8, 2], F32))
            wsl = Slots(K, es2, "wada", [128, 8, D], F32, 2)
            pm_s = Slots(K, es2, "pm", [128, 4, 8, 2], F32, 1, psum=True)
            pm, b_pm = pm_s.next()
            pgt = Slots(K, es2, "pgt", [128, 512], F32, 2, psum=True)
            act(P, sc[:], cpp[:], AF.Silu, [bc], [bc])
            for s in range(2):
                for kc in range(8):
                    ts(P, scb[:, s, kc, :], K.ones_f[:], sc[:, s, kc:kc + 1], None, ALU.mult, None, [bc], [bc])
            vmap = {0: 0, 1: 1, 3: 2, 4: 3}
            for v in range(6):
                wt, wb = wsl.next()
                for hh in range(2):
                    P.dma("sp", wt[:, hh * 4:(hh + 1) * 4, :],
                          d["w_ada"][hh * 512:(hh + 1) * 512, v * D:(v + 1) * D].rearrange("(c p) n -> p c n", p=128),
                          writes=[wb])
                if v in vmap:
                    vi = vmap[v]
                    groups = []
                    for fc in range(8):
                        groups.append((pm[:, vi, fc, :],
                                       [(wt[:, kc, fc * 128:(fc + 1) * 128], sc[:, :, kc]) for kc in range(8)]))
                    mm_multi(P, groups, [wb, bc], [b_pm])
                else:
                    gi = 0 if v == 2 else 1
                    for s in range(2):
                        for half in range(2):
                            pt, pb = pgt.next()
                            pairs = [(scb[:, s, kc, :], wt[:, kc, half * 512:(half + 1) * 512]) for kc in range(8)]
                            pairs.append((K.ones_f[0:1, :], bada_row[0:1, v * D + half * 512: v * D + (half + 1) * 512]))
                            mm(P, pt[:], pairs, [wb, bc], [pb])
                            cp(P, K.gate_bc[:, s, gi, half * 512:(half + 1) * 512], pt[:], [pb], [bc])
            cp(P, modpp[:], pm[:], [b_pm], [bc])
            for s in range(2):
                for (vi, v, dst, g) in ((0, 0, K.B1, None), (1, 1, K.A1, g1), (2, 3, K.B2, None), (3, 4, K.A2, g2)):
                    tt(P, dst[:, s, :], modpp[:, vi, :, s], bada_pp[:, v * 8:(v + 1) * 8], ALU.add, [bc], [bc])
                    if g is not None:
                        stt(P, dst[:, s, :], dst[:, s, :], 1.0, g[:], ALU.add, ALU.mult, [bc], [bc])
            P.barrier()
            P.emit()


def norm_to_hT(K, xt, bx, work, s, A, B, hT_dst, b_hT, col0):
    P = K.P
    bc = K.b_const
    sq, bsq = work["sq"].next()
    st, bst = work["st"].next()
    xn, bxn = work["xn"].next()
    pT, bpT = work["pT"].next()
    mset(P, st[:, 0:1], 0.0, [bst])
    act(P, sq[:], xt, AF.Square, [bx], [bsq, bst], accum_out=st[:, 0:1])
    act(P, st[:, 1:2], st[:, 0:1], AF.Ln, [bst, bc], [bst], scale=1.0 / D, bias=K.eps_c[:, 0:1])
    act(P, st[:, 2:3], st[:, 1:2], AF.Exp, [bst], [bst], scale=-0.5)
    ts(P, xn[:], xt, st[:, 2:3], None, ALU.mult, None, [bx, bst], [bxn])
    transposes(P, [(pT[:, kc, :], xn[:, kc * 128:(kc + 1) * 128]) for kc in range(8)], K.ident_b[:], [bxn, bc], [bpT])
    for kc in range(8):
        o = hT_dst[:, kc, col0:col0 + 128]
        if kc % 2 == 0:
            act(P, o, pT[:, kc, :], AF.Identity, [bpT, bc], [b_hT], scale=A[:, s, kc:kc + 1], bias=B[:, s, kc:kc + 1])
        else:
            ts(P, o, pT[:, kc, :], A[:, s, kc:kc + 1], B[:, s, kc:kc + 1], ALU.mult, ALU.add, [bpT, bc], [b_hT])


def norm_work(K, es, pfx, depth=2):
    return {
        "sq": Slots(K, es, pfx + "sq", [128, D], BF16, 2),
        "st": Slots(K, es, pfx + "st", [128, 4], F32, depth + 2),
        "xn": Slots(K, es, pfx + "xn", [128, D], BF16, depth),
        "pT": Slots(K, es, pfx + "pT", [128, 8, 128], BF16, depth, psum=True),
    }


def bias_tile_jobs(K, es2):
    nc, P, d = K.nc, K.P, K.d
    bc = K.b_const
    G = d["G"]
    hs = Slots(K, es2, "hk", [128, 8, 512], F32, 2)
    bo = Slots(K, es2, "bo", [128, 8, 512], F32, 2)
    pf = Slots(K, es2, "pf", [128, 512], F32, 4, psum=True)
    hkd = Slots(K, es2, "hkd", [128, 8, 2, 128], F32, 2)
    bod = Slots(K, es2, "bod", [128, 8, 2, 128], F32, 2)
    jobs = []

    def head_job(s, h):
        ht, hb = hs.next()
        P.dma("sp", ht[:, 0:6, :], bass.AP(G.tensor, h * OHW, [[1, 128], [128, 6], [1, 512]]), writes=[hb])
        P.dma("sp", ht[:, 6, :], bass.AP(G.tensor, h * OHW + (1 + 2 * s) * GW + 640, [[1, 128], [1, 512]]), writes=[hb], group=True)
        P.dma("sp", ht[:, 7, :], bass.AP(G.tensor, h * OHW + (2 + 2 * s) * GW + 0, [[1, 128], [1, 512]]), writes=[hb], group=True)
        ot, ob = bo.next()
        for i in range(8):
            pt, pb = pf.next()
            mm(P, pt[:], [(K.J_f[:], ht[:, i, :])], [hb, bc], [pb])
            cp(P, ot[:, i, :], pt[:], [pb], [ob])
        P.dma("pool", d["BM"][s][h], ot[:], reads=[ob], writes=[K.b_scr], owner=ob)

    def dil_job(g):
        ht, hb = hkd.next()
        for hh in range(8):
            head = 8 + 8 * g + hh
            P.dma("sp", ht[:, hh, :, :], bass.AP(G.tensor, head * OHW + 5 * GW + g * DW, [[1, 128], [128, 2], [1, 128]]), writes=[hb],
                  group=True)
        ot, ob = bod.next()
        for i in range(4):
            pt, pb = pf.next()
            mm(P, pt[:], [(K.J_f[:], ht[:, 2 * i:2 * i + 2, :, :].rearrange("p a b c -> p (a b c)"))], [hb, bc], [pb])
            cp(P, ot[:, 2 * i:2 * i + 2, :, :].rearrange("p a b c -> p (a b c)"), pt[:], [pb], [ob])
        P.dma("pool", d["TB"][g], ot[:], reads=[ob], writes=[K.b_scr], owner=ob)

    for s in range(2):
        for h in range(8):
            jobs.append(lambda s=s, h=h: head_job(s, h))
    for g in range(3):
        jobs.append(lambda g=g: dil_job(g))
    return jobs


def phaseA(K):
    nc, P, d = K.nc, K.P, K.d
    bc = K.b_const
    with ExitStack() as es:
        xs = Slots(K, es, "xa", [128, D], F32, 4)
        hs = Slots(K, es, "hTa", [128, 8, 512], BF16, 3)
        work = norm_work(K, es, "a", depth=3)
        bjobs = bias_tile_jobs(K, es)
        jobs = []
        for s in range(2):
            for c in range(SEQS[s] // 512):
                for j in range(4):
                    jobs.append({"s": s, "c": c, "j": j})
        cur = {}

        def st0(i):
            jb = jobs[i]
            s, c, j = jb["s"], jb["c"], jb["j"]
            if j == 0:
                cur[(s, c)] = hs.next()
            if j == 2 and bjobs:
                bjobs.pop(0)()
            xt, bx = xs.next()
            r0 = SBASE[s] + c * 512 + j * 128
            P.dma("sp", xt[:], d["xseq"][r0:r0 + 128, :], writes=[bx])
            sq, bsq = work["sq"].next()
            st, bst = work["st"].next()
            xn, bxn = work["xn"].next()
            mset(P, st[:, 0:1], 0.0, [bst])
            act(P, sq[:], xt[:], AF.Square, [bx], [bsq, bst], accum_out=st[:, 0:1])
            act(P, st[:, 1:2], st[:, 0:1], AF.Ln, [bst, bc], [bst], scale=1.0 / D, bias=K.eps_c[:, 0:1])
            act(P, st[:, 2:3], st[:, 1:2], AF.Exp, [bst], [bst], scale=-0.5)
            ts(P, xn[:], xt[:], st[:, 2:3], None, ALU.mult, None, [bx, bst], [bxn])
            jb["xn"] = (xn, bxn)

        def st1(i):
            jb = jobs[i]
            xn, bxn = jb["xn"]
            pT, bpT = work["pT"].next()
            transposes(P, [(pT[:, kc, :], xn[:, kc * 128:(kc + 1) * 128]) for kc in range(8)], K.ident_b[:], [bxn, bc], [bpT])
            jb["pT"] = (pT, bpT)

        def st2(i):
            jb = jobs[i]
            s, c, j = jb["s"], jb["c"], jb["j"]
            pT, bpT = jb["pT"]
            ht, hb = cur[(s, c)]
            for kc in range(8):
                o = ht[:, kc, j * 128:(j + 1) * 128]
                if j % 2 == 0:
                    act(P, o, pT[:, kc, :], AF.Identity, [bpT, bc], [], pw=[hb], scale=K.A1[:, s, kc:kc + 1], bias=K.B1[:, s, kc:kc + 1])
                else:
                    ts(P, o, pT[:, kc, :], K.A1[:, s, kc:kc + 1], K.B1[:, s, kc:kc + 1], ALU.mult, ALU.add, [bpT, bc], [], pw=[hb])
            if j == 3:
                P.dma("pool", d["hT"][s][:, :, c * 512:(c + 1) * 512], ht[:], reads=[hb], writes=[K.b_scr], owner=hb)

        run_pipeline(len(jobs), [st0, None, st1, None, st2])
        while bjobs:
            bjobs.pop(0)()
        P.barrier(include_bg=True)
        P.emit()


def phaseB(K):
    nc, P, d = K.nc, K.P, K.d
    bc = K.b_const
    w_in = d["w_in"]
    with ExitStack() as es:
        wsl = Slots(K, es, "wb", [128, 8, D], BF16, 2)
        hsl = Slots(K, es, "hTb", [128, 8, 512], BF16, 3)
        pp = Slots(K, es, "ppb", [128, 512], F32, 3, psum=True)
        pss = Slots(K, es, "pss", [128, 512], F32, 2, psum=True)
        sqs = Slots(K, es, "sqb", [128, 512], BF16, 2)
        lns = Slots(K, es, "lnb", [128, 512], F32, 2)
        ost = Slots(K, es, "ostb", [128, 512], BF16, 6)
        vas = Slots(K, es, "vas", [128, 8, 130], BF16, 4)
        dvs = Slots(K, es, "dvs", [128, 8, 66], BF16, 4)
        gst = Slots(K, es, "gst", [128, 512], F32, 5)
        for t, b in zip(vas.t, vas.b):
            mset(P, t[:, :, 128:130], 1.0, [b])

        WB = d["WB"]

        def load_w(wt, wb, col0, n, dst0=0):
            for hh in range(2):
                P.dma("sp", wt[:, hh * 4:(hh + 1) * 4, dst0:dst0 + n],
                      WB[hh * 512:(hh + 1) * 512, col0:col0 + n].rearrange("(c p) n -> p c n", p=128), reads=[K.b_wscr], writes=[wb], group=True)

        def load_w_pairs(wt, wb, colA, colB):
            load_w(wt, wb, 0 if colA == 0 else 1024, 1024)

        def load_h(s, c):
            ht, hb = hsl.next()
            P.dma("sp", ht[:], d["hT"][s][:, :, c * 512:(c + 1) * 512], writes=[hb])
            return ht, hb

        jobs = []
        slabs = []

        def add_slab(loader):
            slabs.append(loader)
            return len(slabs) - 1

        NCH = [SEQS[s] // 512 for s in range(2)]
        WCH = [[(c % NCH[s]) for c in range(-2, 6)] for s in range(2)]
        sl = add_slab(lambda wt, wb: load_w_pairs(wt, wb, 1024, 1536))
        for s in range(2):
            for c in range(NCH[s]):
                for h in range(8):
                    jobs.append({"k": "fm", "sl": sl, "s": s, "c": c, "tcol": h * 128, "gain": 1,
                                 "dst": d["KT"][s][h, :, c * 512:(c + 1) * 512]})
        sl = add_slab(lambda wt, wb: load_w(wt, wb, 2048, 1024))
        for s in range(2):
            for c in range(NCH[s]):
                for j in range(4):
                    for half in range(2):
                        jobs.append({"k": "va", "sl": sl, "s": s, "c": c, "j": j, "half": half})
        sl = add_slab(lambda wt, wb: load_w_pairs(wt, wb, 0, 512))
        for s in range(2):
            for c in range(4):
                for h in range(8):
                    jobs.append({"k": "fm", "sl": sl, "s": s, "c": c, "tcol": h * 128, "gain": 0,
                                 "dst": d["QT"][s][h, :, c * 512:(c + 1) * 512]})
        for g in range(3):
            base = 3072 + 1536 * g
            sl = add_slab(lambda wt, wb, base=base: load_w(wt, wb, base, 1024))
            for s in range(2):
                for wi, c in enumerate(WCH[s]):
                    for hp in range(4):
                        jobs.append({"k": "fm", "sl": sl, "s": s, "c": c, "tcol": 512 + hp * 128, "gain": 3 + 2 * g,
                                     "dst": d["DK"][s][g, hp, :, wi * 512:(wi + 1) * 512]})
                        if 2 <= wi < 6:
                            jobs.append({"k": "fm", "sl": sl, "s": s, "c": c, "tcol": hp * 128, "gain": 2 + 2 * g,
                                         "dst": d["DQ"][s][g, hp, :, (wi - 2) * 512:(wi - 1) * 512]})
            sl = add_slab(lambda wt, wb, base=base: load_w(wt, wb, base + 1024, 512))
            for s in range(2):
                for wi, c in enumerate(WCH[s]):
                    for j in range(4):
                        jobs.append({"k": "dv", "sl": sl, "s": s, "c": c, "j": j, "wi": wi, "g": g})
        for gi in range(2):
            sl = add_slab(lambda wt, wb, gi=gi: load_w(wt, wb, 7680 + gi * 1024, 1024))
            for s in range(2):
                for c in range(4):
                    for j in range(4):
                        for half in range(2):
                            jobs.append({"k": "gt", "sl": sl, "s": s, "c": c, "j": j, "half": half, "gi": gi})

        slab_t = {}
        chunk_t = {}
        state = {"va": None}

        def get_slab(sl):
            if sl not in slab_t:
                wt, wb = wsl.next()
                slabs[sl](wt, wb)
                slab_t[sl] = (wt, wb)
            return slab_t[sl]

        ckeys = []
        for jb in jobs:
            key = (jb["sl"], jb["s"], jb["c"])
            if not ckeys or ckeys[-1] != key:
                ckeys.append(key)
            jb["ck"] = len(ckeys) - 1

        def get_chunk(ck):
            for k_ in (ck, ck + 1):
                if k_ < len(ckeys) and k_ not in chunk_t:
                    chunk_t[k_] = load_h(ckeys[k_][1], ckeys[k_][2])
            chunk_t.pop(ck - 1, None)
            return chunk_t[ck]

        def st0(i):
            jb = jobs[i]
            wt, wb = get_slab(jb["sl"])
            if jb["sl"] + 1 < len(slabs) and (i == 0 or jobs[i - 1]["sl"] != jb["sl"]):
                pass
            ht, hb = get_chunk(jb["ck"])
            pt, pb = pp.next()
            k = jb["k"]
            if k == "fm":
                tcol = jb["tcol"]
                mm(P, pt[:], [(wt[:, kc, tcol:tcol + 128], ht[:, kc, :]) for kc in range(8)], [wb, hb], [pb])
            elif k == "dv":
                j = jb["j"]
                mm(P, pt[:], [(ht[:, kc, j * 128:(j + 1) * 128], wt[:, kc, 0:512]) for kc in range(8)], [wb, hb], [pb])
            else:
                j, half = jb["j"], jb["half"]
                mm(P, pt[:], [(ht[:, kc, j * 128:(j + 1) * 128], wt[:, kc, half * 512:(half + 1) * 512]) for kc in range(8)],
                   [wb, hb], [pb])
            jb["pp"] = (pt, pb)
            if i + 1 < len(jobs) and jb["sl"] + 1 < len(slabs) and (i == 0 or jobs[i - 1]["sl"] != jb["sl"]):
                get_slab(jb["sl"] + 1)

        def st1(i):
            jb = jobs[i]
            pt, pb = jb["pp"]
            k = jb["k"]
            s = jb["s"]
            if k == "fm":
                sq, bsq = sqs.next()
                act(P, sq[:], pt[:], AF.Square, [pb], [bsq])
                ps_, bps = pss.next()
                mm(P, ps_[:], [(K.blk_b[:], sq[:])], [bsq, bc], [bps])
                jb["ps"] = (ps_, bps)
            elif k == "va":
                j, half, c = jb["j"], jb["half"], jb["c"]
                if half == 0:
                    state["va"] = vas.next()
                va, bva = state["va"]
                src = pt[:].rearrange("p (h e) -> p h e", e=128)
                if half == 0:
                    act(P, va[:, 0:4, 0:128], src, AF.Copy, [pb], [], pw=[bva])
                else:
                    cp(P, va[:, 4:8, 0:128], src, [pb], [], pw=[bva])
                    r0 = c * 512 + j * 128
                    P.dma("pool", d["VA"][s][r0:r0 + 128, :], va[:].rearrange("p h e -> p (h e)"), reads=[bva], writes=[K.b_scr], owner=bva)
            elif k == "dv":
                j, wi, g = jb["j"], jb["wi"], jb["g"]
                dv, bdv = dvs.next()
                vcol = K.valid_c[:, s, wi * 4 + j: wi * 4 + j + 1]
                ts(P, dv[:, :, 0:64], pt[:].rearrange("p (h e) -> p h e", e=64), vcol, None, ALU.mult, None, [pb, bc], [bdv])
                ts(P, dv[:, :, 64:66], K.ones_f[:, 0:16].rearrange("p (h e) -> p h e", e=2), vcol, None, ALU.mult, None, [bc], [bdv],
                   eng="dve")
                r0 = wi * 512 + j * 128
                P.dma("pool", d["DV"][s][g, r0:r0 + 128, :], dv[:].rearrange("p h e -> p (h e)"), reads=[bdv], writes=[K.b_scr], owner=bdv)
            else:
                j, half, c, gi = jb["j"], jb["half"], jb["c"], jb["gi"]
                go, bgo = gst.next()
                act(P, go[:], pt[:], AF.Sigmoid, [pb], [bgo])
                r0 = c * 512 + j * 128
                cc = gi * 1024 + half * 512
                P.dma("pool", d["GT"][s][r0:r0 + 128, cc:cc + 512], go[:], reads=[bgo], writes=[K.b_scr], owner=bgo)

        def st2(i):
            jb = jobs[i]
            if jb["k"] != "fm":
                return
            pt, pb = jb["pp"]
            ps_, bps = jb["ps"]
            ln, bln = lns.next()
            act(P, ln[:], ps_[:], AF.Ln, [bps, bc], [bln], scale=1.0 / 64, bias=K.eps_c[:, 0:1])
            act(P, ln[:], ln[:], AF.Exp, [bln], [bln], scale=-0.5)
            o, bo = ost.next()
            g = jb["gain"]
            stt(P, o[:], pt[:], K.gains[:, g:g + 1], ln[:], ALU.mult, ALU.mult, [pb, bln, bc], [bo])
            P.dma("pool", jb["dst"], o[:], reads=[bo], writes=[K.b_scr], owner=bo)

        run_pipeline(len(jobs), [st0, st1, st2])
        P.barrier()
        P.emit()


def run_pipeline(njobs, stages, deferred=None):
    ns = len(stages)
    for tick in range(njobs + ns - 1 + 24):
        for st in range(ns):
            j = tick - st
            if 0 <= j < njobs and stages[st] is not None:
                stages[st](j)
        if deferred is not None:
            for fn in deferred.pop(tick, []):
                fn()
    assert not deferred, deferred.keys()


def phaseC(K):
    nc, P, d = K.nc, K.P, K.d
    bc = K.b_const
    G = d["G"]
    with ExitStack() as es:
        kts = Slots(K, es, "ktc", [128, 8192], BF16, 2)
        qts = Slots(K, es, "qtc", [128, 2, OWN], BF16, 2)
        for t_, b_ in zip(qts.t, qts.b):
            mset(P, t_[64:128, 0, :], 0.0, [b_], eng="pool")
            mset(P, t_[0:64, 1, :], 0.0, [b_], eng="pool")
        vts = Slots(K, es, "vac", [128, 64, 130], BF16, 2)
        bms = Slots(K, es, "bmx", [128, 8, 512], F32, 2)
        ps12 = Slots(K, es, "ps12", [128, 1024], F32, 2, psum=True)
        pacc_s = Slots(K, es, "pacc", [128, 3, 130], F32, 3, psum=True)
        pacc = pacc_s.t
        b_acc = P.buf("pacc")
        ptr = Slots(K, es, "ptrc", [128, 128], BF16, 1, psum=True)
        p12s = Slots(K, es, "p12s", [128, 1024], BF16, 5)
        tm12 = Slots(K, es, "tm12", [128, 1024], F32, 3)
        accs = Slots(K, es, "accs", [128, 9, 130], F32, 2)
        fin = Slots(K, es, "finc", [128, 20], F32, 8)
        ofs = Slots(K, es, "ofs", [128, 128], F32, 4)
        of2 = Slots(K, es, "of2", [128, 128], F32, 2)
        rds = Slots(K, es, "rdc", [128, 8], F32, 2)
        osq = Slots(K, es, "osq", [128, 128], BF16, 2)
        onb = Slots(K, es, "onb", [128, 128], BF16, 8)

        yst = Slots(K, es, "ystc", [128, 512], BF16, 2)
        jobs = []
        for s in range(2):
            n = SEQS[s] // 128
            for h in range(8):
                if DBG.get("c_heads") is not None and s * 8 + h >= DBG["c_heads"]:
                    continue
                for t in range(4):
                    far_ = list(range(6, n))
                    hf = len(far_) // 2
                    order = far_[:hf] + list(range(6)) + far_[hf:]
                    assert sorted(order) == list(range(n))
                    for pos, jp in enumerate(order):
                        jobs.append({"s": s, "h": h, "t": t, "jp": jp, "n": n, "kb": (4 * t - 1 + jp) % n,
                                     "first": pos == 0, "last": pos == n - 1})
        head = {}
        deferred = {}

        def head_prologue(s, h):
            S = SEQS[s]
            n = S // 128
            kt, bkt = kts.next()
            qt, bqt = qts.next()
            va, bva = vts.next()
            P.dma("sp", kt[:, 0:S], d["KT"][s][h, :, :], writes=[bkt])
            P.dma("sp", qt[0:64, 0, :], d["QT"][s][h, 0:64, :], writes=[bqt])
            P.dma("sp", qt[64:128, 1, :], d["QT"][s][h, 64:128, :], writes=[bqt])
            for k0 in range(0, n, 8):
                P.dma("sp", va[:, k0:k0 + 8, :],
                      d["VA"][s][k0 * 128:(k0 + 8) * 128, h * 130:(h + 1) * 130].rearrange("(kb p) c -> p kb c", p=128), writes=[bva],
                      group=True)
            bm, bbm = bms.next()
            P.dma("sp", bm[:], d["BM"][s][h], writes=[bbm])
            head[(s, h)] = (kt, bkt, qt, bqt, va, bva, bm, bbm)

        def st_qk(j):
            jb = jobs[j]
            s, h, t, jp, kb = jb["s"], jb["h"], jb["t"], jb["jp"], jb["kb"]
            if t == 0 and jb["first"]:
                head_prologue(s, h)
            kt, bkt, qt, bqt, va, bva, bmx, b_bmx = head[(s, h)]
            q0 = t * 512
            s12, bs12 = ps12.next()
            mm_raw(P, [(s12[:, 0:512], kt[:, kb * 128:(kb + 1) * 128], qt[:, 0, q0:q0 + 512], True, True),
                       (s12[:, 512:1024], kt[:, kb * 128:(kb + 1) * 128], qt[:, 1, q0:q0 + 512], True, True)],
                   [bkt, bqt], [bs12])
            jb["S"] = (s12, bs12)

        def st_exp(j):
            jb = jobs[j]
            s, h, t, jp, kb, n = jb["s"], jb["h"], jb["t"], jb["jp"], jb["kb"], jb["n"]
            s12, bs12 = jb["S"]
            bmx, b_bmx = head[(s, h)][6], head[(s, h)][7]
            p12, bp12 = p12s.next()
            if jp < 6:
                if t == 0 and jp == 0:
                    bt = bmx[:, 6, :]
                elif t == 3 and jp == 5:
                    bt = bmx[:, 7, :]
                else:
                    bt = bmx[:, 5 - jp, :]
                t12, bt12 = tm12.next()
                stt(P, t12[:, 0:512], s12[:, 0:512], 0.125, bt, ALU.mult, ALU.add, [bs12, b_bmx], [bt12])
                stt(P, t12[:, 512:1024], s12[:, 512:1024], 0.125, bt, ALU.mult, ALU.add, [bs12, b_bmx], [bt12])
                act(P, p12[:], t12[:], AF.Exp, [bt12], [bp12])
            else:
                col = t * n + kb if s == 0 else 128 + t * n + kb
                fb = K.farB[:, h, col:col + 1]
                act(P, p12[:], s12[:], AF.Exp, [bs12, bc], [bp12], scale=0.125, bias=fb)
            jb["P"] = (p12, bp12)

        def st_pv(j):
            jb = jobs[j]
            s, h, t, jp, kb, n = jb["s"], jb["h"], jb["t"], jb["jp"], jb["kb"], jb["n"]
            va, bva = head[(s, h)][4], head[(s, h)][5]
            p12, bp12 = jb["P"]
            items = []
            for m in range(2):
                for qb in range(4):
                    a = m * 4 + qb
                    items.append((pacc[a // 3][:, a % 3, :], p12[:, m * 512 + qb * 128:m * 512 + (qb + 1) * 128], va[:, kb, :],
                                  jb["first"] and a % 3 == 0, jb["last"]))
            mm_raw(P, items, [bp12, bva], [b_acc], skip=True)
            jb.pop("S", None)
            jb.pop("P", None)
            if jb["last"]:
                finalize(j, s, h, t)

        def finalize(j, s, h, t):
            ac, bac = accs.next()
            for b in range(3):
                nb = 3 if b < 2 else 2
                cp(P, ac[:, 3 * b:3 * b + nb, :], pacc[b][:, 0:nb, :], [b_acc], [bac])
            ys, bys = yst.next()
            tick0 = j + 2

            def part_a():
                st = []
                rd, brd = rds.next()
                P.op("dve", lambda e: e.reciprocal(out=rd[:], in_=ac[:, 0:8, 128]), [bac], [brd])
                for qb in range(4):
                    f, bf_ = fin.next()
                    of, bof = ofs.next()
                    o2, bo2 = of2.next()
                    ts(P, of[:], ac[:, qb, 0:128], rd[:, qb:qb + 1], None, ALU.mult, None, [bac, brd], [bof])
                    ts(P, o2[:], ac[:, 4 + qb, 0:128], rd[:, 4 + qb:5 + qb], K.neglam[:, 0:1], ALU.mult, ALU.mult, [bac, brd, bc], [bo2],
                       eng="dve")
                    tt(P, of[:], of[:], o2[:], ALU.add, [bo2, bof], [bof])
                    if qb == 0:
                        mset(P, f[:, 8:12], 0.0, [bf_])
                    st.append((f, bf_, of, bof))
                return st

            def part_b(st):
                f0, bf0 = st[0][0], st[0][1]
                for qb, (f, bf_, of, bof) in enumerate(st):
                    sq, bsq = osq.next()
                    act(P, sq[:], of[:], AF.Square, [bof], [bsq, bf0], accum_out=f0[:, 8 + qb:9 + qb])
                act(P, f0[:, 12:16], f0[:, 8:12], AF.Ln, [bf0, bc], [bf0], scale=1.0 / 128, bias=K.eps_c[:, 0:1])
                act(P, f0[:, 16:20], f0[:, 12:16], AF.Exp, [bf0], [bf0], scale=-0.5)

            def part_c(st):
                ons = []
                for qb, (f, bf_, of, bof) in enumerate(st):
                    f0, bf0 = st[0][0], st[0][1]
                    on, bon = onb.next()
                    stt(P, on[:], of[:], f0[:, 16 + qb:17 + qb], K.sg_bc[:], ALU.mult, ALU.mult, [bof, bf0, bc], [bon])
                    ons.append((on, bon))
                return ons

            def part_d(ons):
                for qb, (on, bon) in enumerate(ons):
                    pt, pb = ptr.next()
                    transposes(P, [(pt[:], on[:])], K.ident_b[:], [bon, bc], [pb])
                    cp(P, ys[:, qb * 128:(qb + 1) * 128], pt[:], [pb], [], pw=[bys])
                P.dma("pool", d["YA"][s][h, :, t * 512:(t + 1) * 512], ys[:], reads=[bys], writes=[K.b_scr], owner=bys)

            box = {}
            deferred.setdefault(tick0 + 2, []).append(lambda: box.__setitem__("st", part_a()))
            deferred.setdefault(tick0 + 5, []).append(lambda: part_b(box["st"]))
            deferred.setdefault(tick0 + 8, []).append(lambda: box.__setitem__("ons", part_c(box["st"])))
            deferred.setdefault(tick0 + 11, []).append(lambda: part_d(box["ons"]))

        run_pipeline(len(jobs), [st_qk, st_exp, st_pv], deferred)
        P.barrier()
        P.emit()


def phaseD(K):
    nc, P, d = K.nc, K.P, K.d
    bc = K.b_const
    G = d["G"]
    with ExitStack() as es:
        dq = es.enter_context(nc.sbuf_tensor("dq", [128, 4, OWN], BF16))
        dk = es.enter_context(nc.sbuf_tensor("dk", [128, 4, 4096], BF16))
        dv = es.enter_context(nc.sbuf_tensor("dv", [128, 32, 528], BF16))
        b_dq, b_dk, b_dv = P.buf("dq"), P.buf("dk"), P.buf("dv")
        tb = es.enter_context(nc.sbuf_tensor("tbd", [128, 8, 2, 128], F32))
        b_tb = P.buf("tbd")
        psAB = Slots(K, es, "psAB", [128, 2, 512], F32, 3, psum=True)
        pso = Slots(K, es, "pso", [128, 2, 66], F32, 2, psum=True)
        tmp = Slots(K, es, "tmpd", [128, 2, 256], F32, 2)
        pds = Slots(K, es, "pds", [128, 2, 256], BF16, 4)
        ost = Slots(K, es, "ostd", [128, 8, 66], F32, 4)
        jobs = []
        st_o = {}

        def group_prologue(s, g):
            r = DIL_R[g]
            nkt = 16 // r + 1
            P.dma("sp", dq[:], d["DQ"][s][g].rearrange("hp p t -> p hp t"), writes=[b_dq])
            P.dma("sp", dk[:], d["DK"][s][g].rearrange("hp p t -> p hp t"), writes=[b_dk])
            for c in range(r):
                row0 = 1024 + c - 64 * r
                src = bass.AP(d["DV"][s].tensor, (g * 4096 + row0) * 528, [[r * 528, 128], [128 * r * 528, nkt], [1, 528]])
                P.dma("sp", dv[:, c * nkt:(c + 1) * nkt, :], src, writes=[b_dv], group=True)
            P.dma("sp", tb[:], d["TB"][g], writes=[b_tb])

        def st0(i):
            jb = jobs[i]
            s, g, c, bi, hp = jb["s"], jb["g"], jb["c"], jb["bi"], jb["hp"]
            r = DIL_R[g]
            if c == 0 and bi == 0 and hp == 0:
                group_prologue(s, g)
            q_lo = c + r * 128 * bi
            kA = 1024 + c - 64 * r + 128 * r * bi
            kB = kA + 128 * r
            sAB, bsAB = psAB.next()
            items = []
            for e_ in range(2):
                pr = slice(64 * e_, 64 * e_ + 64)
                qa = dq[pr, hp, q_lo:q_lo + 127 * r + 1:r]
                items.append((sAB[:, e_, 0:128], dk[pr, hp, kB:kB + 127 * r + 1:r], qa, True, True))
                items.append((sAB[:, e_, 128:256], dk[pr, hp, kA:kA + 127 * r + 1:r], qa, True, True))
            mm_raw(P, items, [b_dq, b_dk], [bsAB])
            jb["S"] = (sAB, bsAB)

        def st1(i):
            jb = jobs[i]
            hp = jb["hp"]
            sAB, bsAB = jb["S"]
            t_, bt_ = tmp.next()
            stt(P, t_[:], sAB[:, :, 0:256], 0.125, tb[:, 2 * hp:2 * hp + 2, :, :].rearrange("p h a b -> p h (a b)"), ALU.mult, ALU.add,
                [bsAB, b_tb], [bt_])
            pd, bpd = pds.next()
            act(P, pd[:], t_[:], AF.Exp, [bt_], [bpd])
            jb["P"] = (pd, bpd)

        def st2(i):
            jb = jobs[i]
            s, g, c, bi, hp = jb["s"], jb["g"], jb["c"], jb["bi"], jb["hp"]
            r = DIL_R[g]
            nkt = 16 // r + 1
            if hp == 0:
                st_o["o"] = ost.next()
            o, bo = st_o["o"]
            po, bpo = pso.next()
            tA = c * nkt + bi
            pd, bpd = jb["P"]
            for e_ in range(2):
                hh = 2 * hp + e_
                mm(P, po[:, e_, :], [(pd[:, e_, 0:128], dv[:, tA + 1, hh * 66:(hh + 1) * 66]),
                                     (pd[:, e_, 128:256], dv[:, tA, hh * 66:(hh + 1) * 66])], [bpd, b_dv], [bpo])
            cp(P, o[:, 2 * hp:2 * hp + 2, :], po[:], [bpo], [], eng="dve", pw=[bo])
            if hp == 3:
                q_lo = c + r * 128 * bi
                dst = bass.AP(d["YB"][s].tensor, (g * OWN + q_lo) * 528, [[r * 528, 128], [1, 528]])
                P.dma("pool", dst, o[:].rearrange("p h e -> p (h e)"), reads=[bo], writes=[K.b_scr], owner=bo)

        for s in range(2):
            for g in range(3):
                r = DIL_R[g]
                jobs.clear()
                for c in range(r):
                    for bi in range(16 // r):
                        for hp in range(4):
                            jobs.append({"s": s, "g": g, "c": c, "bi": bi, "hp": hp})
                run_pipeline(len(jobs), [st0, None, st1, None, st2])
        P.barrier()
        P.emit()


def phaseE1(K):
    nc, P, d = K.nc, K.P, K.d
    bc = K.b_const
    with ExitStack() as es:
        wa = es.enter_context(nc.sbuf_tensor("wa", [128, 8, D], BF16))
        wbb = es.enter_context(nc.sbuf_tensor("wbb", [128, 4, D], BF16))
        wo = es.enter_context(nc.sbuf_tensor("wo", [128, 8, D], BF16))
        b_w = P.buf("we1")
        P.dma("sp", wa[:], d["WA"].rearrange("(c p) n -> p c n", p=128), reads=[K.b_wscr], writes=[b_w])
        P.dma("sp", wbb[:], d["WBB"].rearrange("(c p) n -> p c n", p=128), reads=[K.b_wscr], writes=[b_w])
        P.dma("sp", wo[:], d["WO"].rearrange("(c p) n -> p c n", p=128), reads=[K.b_wscr], writes=[b_w])
        yas = Slots(K, es, "yas", [128, 8, 128], BF16, 2)
        ybs = Slots(K, es, "ybs", [128, 3, 528], F32, 2)
        gts = Slots(K, es, "gts", [128, 2 * D], F32, 2)
        xs = Slots(K, es, "xe", [128, D], F32, 3)
        ybn = Slots(K, es, "ybn", [128, 512], BF16, 2)
        ybT = Slots(K, es, "ybT", [128, 4, 128], BF16, 2)
        rds = Slots(K, es, "rds", [128, 8], F32, 2)
        m1s = Slots(K, es, "m1s", [128, 512], F32, 2)
        m2s = Slots(K, es, "m2s", [128, 512], F32, 2)
        mg = Slots(K, es, "mg", [128, D], BF16, 2)
        mT = Slots(K, es, "mT", [128, 8, 128], BF16, 2)
        h2 = Slots(K, es, "h2", [128, 8, 512], BF16, 2)
        work = norm_work(K, es, "e")
        pa = Slots(K, es, "pae", [128, 512], F32, 2, psum=True)
        pb_ = Slots(K, es, "pbe", [128, 512], F32, 2, psum=True)
        pt4 = Slots(K, es, "pt4", [128, 8, 128], BF16, 2, psum=True)
        def tile_gen(s, c, j, ht, hb):
            t0 = c * 512 + j * 128
            ya, bya = yas.next()
            yb, byb = ybs.next()
            gt, bgt = gts.next()
            xt, bx = xs.next()
            P.dma("sp", ya[:], d["YA"][s][:, :, t0:t0 + 128].rearrange("h v t -> v h t"), writes=[bya])
            P.dma("sp", yb[:], d["YB"][s][:, t0:t0 + 128, :].rearrange("g t e -> t g e"), writes=[byb])
            P.dma("sp", gt[:], d["GT"][s][t0:t0 + 128, :], writes=[bgt])
            P.dma("sp", xt[:], d["xseq"][SBASE[s] + t0:SBASE[s] + t0 + 128, :], writes=[bx])
            tt(P, yb[:, 0, :], yb[:, 0, :], yb[:, 1, :], ALU.add, [byb], [byb])
            tt(P, yb[:, 0, :], yb[:, 0, :], yb[:, 2, :], ALU.add, [byb], [byb])
            rd, brd = rds.next()
            ybv = yb[:, 0, :].rearrange("p (h e) -> p h e", e=66)
            P.op("dve", lambda e, rd=rd, ybv=ybv: e.reciprocal(out=rd[:], in_=ybv[:, :, 64]), [byb], [brd])
            yield
            yn, byn = ybn.next()
            for hh in range(8):
                if hh % 2 == 0:
                    act(P, yn[:, hh * 64:(hh + 1) * 64], ybv[:, hh, 0:64], AF.Copy, [byb, brd], [], pw=[byn], scale=rd[:, hh:hh + 1])
                else:
                    ts(P, yn[:, hh * 64:(hh + 1) * 64], ybv[:, hh, 0:64], rd[:, hh:hh + 1], None, ALU.mult, None, [byb, brd], [], pw=[byn])
            yield
            p4, bp4 = pt4.next()
            transposes(P, [(p4[:, i, :], yn[:, i * 128:(i + 1) * 128]) for i in range(4)], K.ident_b[:], [byn, bc], [bp4])
            yield
            yt_, byt = ybT.next()
            cp(P, yt_[:], p4[:, 0:4, :], [bp4], [byt])
            yield
            mgt, bmg = mg.next()
            for half in range(2):
                cs = slice(half * 512, (half + 1) * 512)
                pA, bpA = pa.next()
                pB, bpB = pb_.next()
                mm(P, pA[:], [(ya[:, h, :], wa[:, h, cs]) for h in range(8)], [bya, b_w], [bpA])
                mm(P, pB[:], [(yt_[:, hp, :], wbb[:, hp, cs]) for hp in range(4)], [byt, b_w], [bpB])
                yield
                m1, bm1 = m1s.next()
                m2, bm2 = m2s.next()
                tt(P, m1[:], pA[:], gt[:, half * 512:(half + 1) * 512], ALU.mult, [bpA, bgt], [bm1])
                tt(P, m2[:], pB[:], gt[:, D + half * 512:D + (half + 1) * 512], ALU.mult, [bpB, bgt], [bm2])
                tt(P, mgt[:, cs], m1[:], m2[:], ALU.add, [bm1, bm2], [bmg])
                yield
            p8, bp8 = work["pT"].next()
            transposes(P, [(p8[:, kc, :], mgt[:, kc * 128:(kc + 1) * 128]) for kc in range(8)], K.ident_b[:], [bmg, bc], [bp8])
            yield
            mt, bmt = mT.next()
            cp(P, mt[:], p8[:], [bp8], [bmt])
            yield
            for half in range(2):
                cs = slice(half * 512, (half + 1) * 512)
                pA, bpA = pa.next()
                mm(P, pA[:], [(mt[:, kc, :], wo[:, kc, cs]) for kc in range(8)], [bmt, b_w], [bpA])
                yield
                m1, bm1 = m1s.next()
                tt(P, m1[:], pA[:], K.gate_bc[:, s, 0, cs], ALU.mult, [bpA, bc], [bm1])
                tt(P, xt[:, cs], xt[:, cs], m1[:], ALU.add, [bx, bm1], [bx])
                yield
            P.dma("pool", d["X1"][s][t0:t0 + 128, :], xt[:], reads=[bx], writes=[K.b_scr], owner=bx)
            sq, bsq = work["sq"].next()
            st, bst = work["st"].next()
            xn, bxn = work["xn"].next()
            mset(P, st[:, 0:1], 0.0, [bst])
            act(P, sq[:], xt[:], AF.Square, [bx], [bsq, bst], accum_out=st[:, 0:1])
            act(P, st[:, 1:2], st[:, 0:1], AF.Ln, [bst, bc], [bst], scale=1.0 / D, bias=K.eps_c[:, 0:1])
            act(P, st[:, 2:3], st[:, 1:2], AF.Exp, [bst], [bst], scale=-0.5)
            ts(P, xn[:], xt[:], st[:, 2:3], None, ALU.mult, None, [bx, bst], [bxn])
            yield
            pT, bpT = work["pT"].next()
            transposes(P, [(pT[:, kc, :], xn[:, kc * 128:(kc + 1) * 128]) for kc in range(8)], K.ident_b[:], [bxn, bc], [bpT])
            yield
            for kc in range(8):
                o = ht[:, kc, j * 128:(j + 1) * 128]
                if j % 2 == 0:
                    act(P, o, pT[:, kc, :], AF.Identity, [bpT, bc], [], pw=[hb], scale=K.A2[:, s, kc:kc + 1], bias=K.B2[:, s, kc:kc + 1])
                else:
                    ts(P, o, pT[:, kc, :], K.A2[:, s, kc:kc + 1], K.B2[:, s, kc:kc + 1], ALU.mult, ALU.add, [bpT, bc], [], pw=[hb])

        for s in range(2):
            for c in range(4):
                ht, hb = h2.next()
                for j0 in (0, 2):
                    gens = [tile_gen(s, c, j0, ht, hb), tile_gen(s, c, j0 + 1, ht, hb)]
                    while gens:
                        for g_ in list(gens):
                            try:
                                next(g_)
                            except StopIteration:
                                gens.remove(g_)
                P.dma("pool", d["H2"][s][:, :, c * 512:(c + 1) * 512], ht[:], reads=[hb], writes=[K.b_scr], owner=hb)
        P.barrier()
        P.emit()


def phaseE1b(K):
    nc, P, d = K.nc, K.P, K.d
    with ExitStack() as es:
        wup = es.enter_context(nc.sbuf_tensor("wup", [128, 8, 4 * D], BF16))
        b_w = P.buf("we1b")
        for i in range(4):
            P.dma("sp", wup[:, :, i * D:(i + 1) * D], d["WUP"][:, i * D:(i + 1) * D].rearrange("(c p) n -> p c n", p=128), reads=[K.b_wscr], writes=[b_w], group=True)
        h2 = Slots(K, es, "h2b", [128, 8, 512], BF16, 2)
        rl = Slots(K, es, "rl", [128, 512], F32, 2)
        us = Slots(K, es, "us", [128, 512], BF16, 6)
        pa = Slots(K, es, "pau", [128, 512], F32, 4, psum=True)
        for s in range(2):
            for c in range(4):
                ht, hb = h2.next()
                P.dma("sp", ht[:], d["H2"][s][:, :, c * 512:(c + 1) * 512], writes=[hb])
                for f in range(32):
                    pA, bpA = pa.next()
                    mm(P, pA[:], [(wup[:, kc, f * 128:(f + 1) * 128], ht[:, kc, :]) for kc in range(8)], [b_w, hb], [bpA])
                    r_, br = rl.next()
                    act(P, r_[:], pA[:], AF.Relu, [bpA], [br])
                    u, bu = us.next()
                    tt(P, u[:], r_[:], r_[:], ALU.mult, [br], [bu])
                    P.dma("pool", d["UT"][s][f, :, c * 512:(c + 1) * 512], u[:], reads=[bu], writes=[K.b_scr], owner=bu)
        P.barrier()
        P.emit()


def phaseE2(K):
    nc, P, d = K.nc, K.P, K.d
    bc = K.b_const
    with ExitStack() as es:
        wd = es.enter_context(nc.sbuf_tensor("wd", [128, 32, D], BF16))
        b_w = P.buf("we2")
        for i in range(4):
            P.dma("sp", wd[:, i * 8:(i + 1) * 8, :], d["WDN"][i * D:(i + 1) * D, :].rearrange("(c p) n -> p c n", p=128), reads=[K.b_wscr], writes=[b_w], group=True)
        uts = Slots(K, es, "uts", [128, 32, 256], BF16, 2)
        x1s = Slots(K, es, "x1s", [128, D], F32, 4)
        m1s = Slots(K, es, "m1z", [128, 512], F32, 2)
        pa = Slots(K, es, "paz", [128, 512], F32, 4, psum=True)
        for s in range(2):
            for c in range(8):
                ut, but = uts.next()
                for i in range(4):
                    P.dma("sp", ut[:, i * 8:(i + 1) * 8, :], d["UT"][s][i * 8:(i + 1) * 8, :, c * 256:(c + 1) * 256].rearrange("f p t -> p f t"),
                          writes=[but], group=True)
                for j in range(2):
                    t0 = c * 256 + j * 128
                    xt, bx = x1s.next()
                    P.dma("sp", xt[:], d["X1"][s][t0:t0 + 128, :], writes=[bx])
                    for half in range(2):
                        cs = slice(half * 512, (half + 1) * 512)
                        pA, bpA = pa.next()
                        mm(P, pA[:], [(ut[:, f, j * 128:(j + 1) * 128], wd[:, f, cs]) for f in range(32)], [but, b_w], [bpA])
                        m1, bm1 = m1s.next()
                        tt(P, m1[:], pA[:], K.gate_bc[:, s, 1, cs], ALU.mult, [bpA, bc], [bm1])
                        tt(P, xt[:, cs], xt[:, cs], m1[:], ALU.add, [bx, bm1], [bx])
                    P.dma("pool", d["y"][s * OWN + t0:s * OWN + t0 + 128, :], xt[:], reads=[bx], writes=[K.b_out], owner=bx)
        P.barrier()
        P.emit()


WEIGHTS = [("rel_bias", [32, 32]), ("norm1_g", [1, D]), ("w_ada", [D, 6 * D]), ("b_ada", [1, 6 * D]), ("w_in", [D, 9728]),
           ("qn_a", [1, 64]), ("kn_a", [1, 64]), ("lambda_q1", [1, 64]), ("lambda_k1", [1, 64]), ("lambda_q2", [1, 64]),
           ("lambda_k2", [1, 64]), ("subln_g", [1, 128]), ("qn_b", [3, 64]), ("kn_b", [3, 64]), ("w_br_a", [D, D]),
           ("w_br_b", [512, D]), ("w_o", [D, D]), ("norm2_g", [1, D]), ("w_up", [D, 4 * D]), ("w_down", [4 * D, D])]

SCRATCH = {
    "G": ([32, OHW], F32),
}


def build(phases=("0", "A", "B", "C", "D", "E1", "E1b", "E2"), debug=False):
    nc = bass.Bass("TRN2", target_bir_lowering=False)
    K = Ctx()
    K.nc = nc
    P = Prog(nc)
    K.P = P
    d = {}
    inp = lambda name, shape: nc.dram_tensor(name, shape, F32, kind="ExternalInput").ap()
    d["xseq"] = inp("xseq", [12288, D])
    d["c2"] = inp("c2", [2, D])
    d["oh"] = inp("oh", [33, OHW])
    d["ohfar"] = inp("ohfar", [32, 384])
    d["valid"] = inp("valid", [2, 4096])
    for nm, shp in WEIGHTS:
        d[nm] = inp(nm, shp)
    d["y"] = nc.dram_tensor("y", [2 * OWN, D], F32, kind="ExternalOutput").ap()

    def scr(name, shape, dt):
        if debug:
            return nc.dram_tensor(name, shape, dt, kind="ExternalOutput").ap()
        return nc.dram_tensor(name, shape, dt).ap()

    d["G"] = scr("G", [32, OHW], F32)
    d["WB"] = nc.dram_tensor("WB", [D, 9728], BF16).ap()
    d["WA"] = nc.dram_tensor("WA", [D, D], BF16).ap()
    d["WBB"] = nc.dram_tensor("WBB", [512, D], BF16).ap()
    d["WO"] = nc.dram_tensor("WO", [D, D], BF16).ap()
    d["WUP"] = nc.dram_tensor("WUP", [D, 4 * D], BF16).ap()
    d["WDN"] = nc.dram_tensor("WDN", [4 * D, D], BF16).ap()
    d["BM"] = [scr(f"BM{s_}", [8, 128, 8, 512], F32) for s_ in range(2)]
    d["TB"] = scr("TB", [3, 128, 8, 2, 128], F32)
    for k_ in ("hT", "KT", "QT", "VA", "DQ", "DK", "DV", "GT", "YA", "YB", "X1", "UT", "H2"):
        d[k_] = [None, None]
    for s in range(2):
        S = SEQS[s]
        d["hT"][s] = scr(f"hT{s}", [128, 8, S], BF16)
        d["KT"][s] = scr(f"KT{s}", [8, 128, S], BF16)
        d["QT"][s] = scr(f"QT{s}", [8, 128, OWN], BF16)
        d["VA"][s] = scr(f"VA{s}", [S, 8 * 130], BF16)
        d["DQ"][s] = scr(f"DQ{s}", [3, 4, 128, OWN], BF16)
        d["DK"][s] = scr(f"DK{s}", [3, 4, 128, 4096], BF16)
        d["DV"][s] = scr(f"DV{s}", [3, 4096, 528], BF16)
        d["GT"][s] = scr(f"GT{s}", [OWN, 2 * D], F32)
        d["YA"][s] = scr(f"YA{s}", [8, 128, OWN], BF16)
        d["YB"][s] = scr(f"YB{s}", [3, OWN, 528], F32)
        d["X1"][s] = scr(f"X1{s}", [OWN, D], F32)
        d["UT"][s] = scr(f"UT{s}", [32, 128, OWN], BF16)
        d["H2"][s] = scr(f"H2{s}", [128, 8, OWN], BF16)
    K.d = d
    K.b_const = P.buf("const", multi=True)
    K.b_scr = P.buf("scr", multi=True)
    K.b_wscr = P.buf("wscr", multi=True)
    K.b_out = P.buf("out", multi=True)
    sbt = lambda name, shape, dt=F32: nc.alloc_sbuf_tensor(name, shape, dt)
    K.ident_f = sbt("ident_f", [128, 128])
    K.ident_b = sbt("ident_b", [128, 128], BF16)
    K.J_f = sbt("J_f", [128, 128])
    K.ones_f = sbt("ones_f", [128, 128])
    K.blk_f = sbt("blk_f", [128, 128])
    K.blk_b = sbt("blk_b", [128, 128], BF16)
    K.eps_c = sbt("eps_c", [128, 1])
    K.A1 = sbt("A1", [128, 2, 8])
    K.B1 = sbt("B1", [128, 2, 8])
    K.A2 = sbt("A2", [128, 2, 8])
    K.B2 = sbt("B2", [128, 2, 8])
    K.gate_bc = sbt("gate_bc", [128, 2, 2, D])
    K.gains = sbt("gains", [128, 8])
    K.sg_bc = sbt("sg_bc", [128, 128])
    K.neglam = sbt("neglam", [128, 1])
    K.valid_c = sbt("valid_c", [128, 2, 32])
    K.farB = sbt("farB", [128, 8, 384])
    fns = {"0": phase0, "A": phaseA, "B": phaseB, "C": phaseC, "D": phaseD, "E1": phaseE1, "E1b": phaseE1b, "E2": phaseE2}
    for ph in phases:
        fns[ph](K)
    return nc, K


def _rel_bucket(rel):
    nb, max_exact = 16, 8
    rel = np.asarray(rel, dtype=np.int64)
    ret = np.where(rel > 0, nb, 0)
    n = np.abs(rel)
    nf = np.maximum(n, 1).astype(np.float32)
    large = max_exact + (np.log(nf / np.float32(max_exact)) / np.float32(math.log(128 / max_exact))
                         * np.float32(nb - max_exact)).astype(np.int32)
    large = np.minimum(large, nb - 1)
    return (ret + np.where(n < max_exact, n, large)).astype(np.int64)


def _core_tables(core):
    p0 = (2048 * (core % 2), 2048 * (core % 4))
    oh = np.zeros((33, OHW), np.float32)
    m = np.arange(GW)
    bmain = _rel_bucket(639 - m)
    oh[bmain, m] = 1.0
    for s in range(2):
        S = SEQS[s]
        left_edge = p0[s] == 0
        right_edge = p0[s] + OWN == S
        be0 = np.full(GW, 31) if left_edge else bmain
        be5 = np.full(GW, 15) if right_edge else bmain
        oh[be0, (1 + 2 * s) * GW + m] = 1.0
        oh[be5, (2 + 2 * s) * GW + m] = 1.0
    mm_ = np.arange(DW)
    jj = 191 - mm_
    for g, r in enumerate(DIL_R):
        b = _rel_bucket(jj * r)
        b = np.where(np.abs(jj) <= 64, b, 32)
        oh[b, 5 * GW + g * DW + mm_] = 1.0
    ohfar = np.zeros((32, 384), np.float32)
    col = 0
    for s in range(2):
        n = SEQS[s] // 128
        for t in range(4):
            for kb in range(n):
                gk = (p0[s] // 128 + kb) % n
                gq = p0[s] // 128 + 4 * t
                ohfar[15 if gk < gq else 31, col] = 1.0
                col += 1
    valid = np.zeros((2, 4096), np.float32)
    for s in range(2):
        gpos = p0[s] - 1024 + np.arange(4096)
        valid[s] = ((gpos >= 0) & (gpos < SEQS[s])).astype(np.float32)
    return oh, ohfar, valid


def make_in_maps(inputs):
    w = {}
    for nm, shp in WEIGHTS:
        a = np.asarray(inputs[nm], dtype=np.float32)
        if nm != "rel_bias":
            a = a[0]
        w[nm] = np.ascontiguousarray(a.reshape(shp))
    xp = np.asarray(inputs["x_prompt"], dtype=np.float32)
    xs = np.asarray(inputs["x_sample"], dtype=np.float32)
    cp_ = np.asarray(inputs["c_prompt"], dtype=np.float32)
    cs = np.asarray(inputs["c_sample"], dtype=np.float32)
    maps = []
    for core in range(8):
        pb, ph = core // 2, core % 2
        sb_, sq = core // 4, core % 4
        xseq = np.concatenate([np.roll(xp[pb], -2048 * ph, axis=0), np.roll(xs[sb_], -2048 * sq, axis=0)], axis=0)
        oh, ohfar, valid = _core_tables(core)
        m = {"xseq": np.ascontiguousarray(xseq), "c2": np.ascontiguousarray(np.stack([cp_[pb], cs[sb_]])),
             "oh": oh, "ohfar": ohfar, "valid": valid}
        m.update(w)
        maps.append(m)
    return maps


_NC = None


def kernel(**inputs):
    global _NC
    if _NC is None:
        _NC = build()[0]
    maps = make_in_maps(inputs)
    res = run_bass_kernel_spmd(_NC, maps, core_ids=list(range(8)))
    yp = np.zeros((4, 4096, D), np.float32)
    ys = np.zeros((2, 8192, D), np.float32)
    for core in range(8):
        y = res.results[core]["y"]
        pb, ph = core // 2, core % 2
        sb_, sq = core // 4, core % 4
        yp[pb, 2048 * ph:2048 * (ph + 1)] = y[0:OWN]
        ys[sb_, 2048 * sq:2048 * (sq + 1)] = y[OWN:2 * OWN]
    return (yp, ys)
```
